# Optimizing a Trainium2 kernel written in Bass

```python
import math
import jax, jax.numpy as jnp
from jax import lax
import numpy as np

D_MODEL = 1024
BATCH = 16
SEQ = 4096
DEPTH = 4

ATTN_HEADS = 8
ATTN_HEAD_DIM = 64
ATTN_WIDTH = ATTN_HEADS * ATTN_HEAD_DIM
Q_BLOCK = 128
SSM_GROUPS = 32
SSM_GROUP_CH = 16
SSM_WIDTH = SSM_GROUPS * SSM_GROUP_CH
SSM_STATE = 64
D_FF = 4 * D_MODEL
N_IN = 3 * ATTN_WIDTH + ATTN_HEADS + SSM_WIDTH + 2 * D_MODEL
RMS_EPS = 1e-6
DT_MIN = 1e-3
DT_MAX = 1e-1

kernel_name = 'fox_s5_gated_hybrid_trunk'


def rmsnorm(x, g):
    xf = x.astype(jnp.float32)
    xf = xf * lax.rsqrt(jnp.mean(xf * xf, axis=-1, keepdims=True) + RMS_EPS)
    return (xf * g.astype(jnp.float32)).astype(x.dtype)


def forgetting_attention(q, k, v, log_f):
    seq = q.shape[2]
    scale = ATTN_HEAD_DIM ** -0.5
    cum = jnp.cumsum(log_f, axis=-1)
    outs = []
    for i in range(seq // Q_BLOCK):
        lo, hi = i * Q_BLOCK, (i + 1) * Q_BLOCK
        s = jnp.einsum('bhqd,bhkd->bhqk', q[:, :, lo:hi], k[:, :, :hi]).astype(jnp.float32) * scale
        s = s + cum[:, :, lo:hi, None] - cum[:, :, None, :hi]
        causal = (lo + jnp.arange(Q_BLOCK))[:, None] >= jnp.arange(hi)[None, :]
        p = jax.nn.softmax(jnp.where(causal, s, -jnp.inf), axis=-1)
        outs.append(jnp.einsum('bhqk,bhkd->bhqd', p.astype(v.dtype), v[:, :, :hi]))
    return jnp.concatenate(outs, axis=2)


def _linear_recurrence(e1, e2):
    a1, b1 = e1
    a2, b2 = e2
    return a1 * a2, a2 * b1 + b2


def s5_ssm(u, lam_re, lam_im, log_dt, b_re, b_im, c_re, c_im, d_skip):
    bsz, seq, _ = u.shape
    f32 = jnp.float32
    ug = u.astype(f32).reshape(bsz, seq, SSM_GROUPS, SSM_GROUP_CH)
    lam = lax.complex(lam_re.astype(f32), lam_im.astype(f32))
    dt = jnp.exp(log_dt.astype(f32))[:, None]
    lam_bar = jnp.exp(lam * dt)
    b_mat = lax.complex(b_re.astype(f32), b_im.astype(f32))
    b_bar = ((lam_bar - 1.0) / lam)[:, :, None] * b_mat
    bu = jnp.einsum('bsgc,gpc->bsgp', ug.astype(jnp.complex64), b_bar)
    a = jnp.broadcast_to(lam_bar[None, None], (1, seq, SSM_GROUPS, SSM_STATE))
    _, states = lax.associative_scan(_linear_recurrence, (a, bu), axis=1)
    c_mat = lax.complex(c_re.astype(f32), c_im.astype(f32))
    y = jnp.einsum('bsgp,gcp->bsgc', states, c_mat).real
    y = y + d_skip.astype(f32).reshape(SSM_GROUPS, SSM_GROUP_CH) * ug
    return y.reshape(bsz, seq, SSM_WIDTH).astype(u.dtype)


def hybrid_layer(x, norm_mix, w_in, b_forget, lam_re, lam_im, log_dt, b_re, b_im,
                 c_re, c_im, d_skip, w_glu, b_glu, w_branch_a, w_branch_b, w_out,
                 norm_mlp, w_mlp_up, w_mlp_down):
    bsz, seq, _ = x.shape
    h = rmsnorm(x, norm_mix)
    proj = h @ w_in
    o1 = ATTN_WIDTH
    o2 = o1 + ATTN_WIDTH
    o3 = o2 + ATTN_WIDTH
    o4 = o3 + ATTN_HEADS
    o5 = o4 + SSM_WIDTH
    o6 = o5 + D_MODEL
    q, k, v, f_logit, u, gate_a, gate_b = jnp.split(proj, [o1, o2, o3, o4, o5, o6], axis=-1)

    def heads(t):
        return t.reshape(bsz, seq, ATTN_HEADS, ATTN_HEAD_DIM).transpose(0, 2, 1, 3)
    log_f = jax.nn.log_sigmoid((f_logit + b_forget).astype(jnp.float32)).transpose(0, 2, 1)
    y_a = forgetting_attention(heads(q), heads(k), heads(v), log_f)
    y_a = y_a.transpose(0, 2, 1, 3).reshape(bsz, seq, ATTN_WIDTH)

    y_b = jax.nn.gelu(s5_ssm(u, lam_re, lam_im, log_dt, b_re, b_im, c_re, c_im, d_skip))
    y_b = y_b * jax.nn.sigmoid(y_b @ w_glu + b_glu)

    mixed = jax.nn.sigmoid(gate_a) * (y_a @ w_branch_a) + jax.nn.sigmoid(gate_b) * (y_b @ w_branch_b)
    x = x + mixed @ w_out

    h = rmsnorm(x, norm_mlp)
    x = x + jnp.square(jax.nn.relu(h @ w_mlp_up)) @ w_mlp_down
    return x


def setup_inputs(seed: int = 0) -> dict:
    key = jax.random.key(seed)
    ks = jax.random.split(key, 24)
    f32 = jnp.float32
    L, G, P, C = DEPTH, SSM_GROUPS, SSM_STATE, SSM_GROUP_CH

    def nrm(k, shape, scale):
        return jax.random.normal(k, shape, f32) * scale

    n_idx = jnp.arange(P, dtype=f32)
    return {
        'x': nrm(ks[0], (BATCH, SEQ, D_MODEL), 1.0),
        'norm_mix': 1.0 + nrm(ks[1], (L, D_MODEL), 0.02),
        'w_in': nrm(ks[2], (L, D_MODEL, N_IN), D_MODEL ** -0.5),
        'b_forget': jax.random.uniform(ks[3], (L, ATTN_HEADS), f32, 1.0, 5.0),
        'ssm_lambda_re': -0.5 + nrm(ks[4], (L, G, P), 0.01),
        'ssm_lambda_im': jnp.pi * n_idx + nrm(ks[5], (L, G, P), 0.01),
        'ssm_log_dt': jax.random.uniform(ks[6], (L, G), f32, math.log(DT_MIN), math.log(DT_MAX)),
        'ssm_b_re': nrm(ks[7], (L, G, P, C), (2 * C) ** -0.5),
        'ssm_b_im': nrm(ks[8], (L, G, P, C), (2 * C) ** -0.5),
        'ssm_c_re': nrm(ks[9], (L, G, C, P), P ** -0.5),
        'ssm_c_im': nrm(ks[10], (L, G, C, P), P ** -0.5),
        'ssm_d': nrm(ks[11], (L, SSM_WIDTH), 1.0),
        'w_glu': nrm(ks[12], (L, SSM_WIDTH, SSM_WIDTH), SSM_WIDTH ** -0.5),
        'b_glu': nrm(ks[13], (L, SSM_WIDTH), 0.01),
        'w_branch_a': nrm(ks[14], (L, ATTN_WIDTH, D_MODEL), ATTN_WIDTH ** -0.5),
        'w_branch_b': nrm(ks[15], (L, SSM_WIDTH, D_MODEL), SSM_WIDTH ** -0.5),
        'w_out': nrm(ks[16], (L, D_MODEL, D_MODEL), D_MODEL ** -0.5),
        'norm_mlp': 1.0 + nrm(ks[17], (L, D_MODEL), 0.02),
        'w_mlp_up': nrm(ks[18], (L, D_MODEL, D_FF), D_MODEL ** -0.5),
        'w_mlp_down': nrm(ks[19], (L, D_FF, D_MODEL), D_FF ** -0.5),
        'norm_final': 1.0 + nrm(ks[20], (D_MODEL,), 0.02),
    }


def reference(x, norm_mix, w_in, b_forget, ssm_lambda_re, ssm_lambda_im, ssm_log_dt,
              ssm_b_re, ssm_b_im, ssm_c_re, ssm_c_im, ssm_d, w_glu, b_glu,
              w_branch_a, w_branch_b, w_out, norm_mlp, w_mlp_up, w_mlp_down, norm_final):
    for l in range(DEPTH):
        x = hybrid_layer(x, norm_mix[l], w_in[l], b_forget[l], ssm_lambda_re[l], ssm_lambda_im[l],
                         ssm_log_dt[l], ssm_b_re[l], ssm_b_im[l], ssm_c_re[l], ssm_c_im[l],
                         ssm_d[l], w_glu[l], b_glu[l], w_branch_a[l], w_branch_b[l], w_out[l],
                         norm_mlp[l], w_mlp_up[l], w_mlp_down[l])
    return rmsnorm(x, norm_final)
```

```python
import numpy as np
from contextlib import ExitStack
import concourse.bass as bass
import concourse.mybir as mybir
from concourse.bass_utils import run_bass_kernel_spmd

F32 = mybir.dt.float32
BF16 = mybir.dt.bfloat16
AF = mybir.ActivationFunctionType
ALU = mybir.AluOpType

D = 1024
KC = 8
H = 8
DH = 64
AW = 512
SW = 512
G = 32
PS = 64
CG = 16
DFF = 4096
NIN = 4104
O_Q, O_K, O_V, O_F, O_U, O_GA, O_GB = 0, 512, 1024, 1536, 1544, 2056, 3080
LCH = 16
EPS = 1e-6
NCORES = 8
SPC = 2


class Buf:
    __slots__ = ("name", "last_write", "reads", "sem_key")

    def __init__(self, name, sem_key=None):
        self.name = name
        self.last_write = None
        self.reads = []
        self.sem_key = sem_key if sem_key is not None else name


class Eng:
    def __init__(self, name, sem_idx):
        self.name = name
        self.sem_idx = sem_idx
        self.count = 0
        self.ops = []
        self.waited = {}
        self.pending = []


class Prog:
    def __init__(self, nc):
        self.nc = nc
        self.sem_names = []
        self.engs = {}
        for n in ("pe", "act", "dve", "pool", "sp"):
            self.engs[n] = Eng(n, self._new_sem("e_" + n))
        self.dma_sems = {}
        self.dma_counts = {}
        self.same_engine_sync = {"pe": False, "act": True, "dve": True, "pool": True, "sp": False}

    def _new_sem(self, name):
        self.sem_names.append(name)
        return len(self.sem_names) - 1

    def _collect_waits(self, eng, reads, writes):
        waits = {}

        def add(ev):
            if ev is None:
                return
            s, v = ev
            if waits.get(s, 0) < v:
                waits[s] = v
        for b in reads:
            add(b.last_write)
        for b in writes:
            add(b.last_write)
            for r in b.reads:
                add(r)
        out = []
        for s, v in waits.items():
            if s == eng.sem_idx and not self.same_engine_sync[eng.name]:
                continue
            if eng.waited.get(s, 0) >= v:
                continue
            eng.waited[s] = v
            out.append((s, v))
        return out

    def op(self, engname, fn, reads=(), writes=(), signal=True):
        eng = self.engs[engname]
        reads = list(reads)
        writes = list(writes)
        waits = self._collect_waits(eng, reads, writes)
        if signal:
            eng.count += 1
            ev = (eng.sem_idx, eng.count)
            for (r, w) in eng.pending + [(reads, writes)]:
                for b in r:
                    b.reads.append(ev)
                for b in w:
                    b.last_write = ev
                    b.reads = []
            eng.pending = []
        else:
            ev = None
            eng.pending.append((reads, writes))
        eng.ops.append((waits, fn, eng.sem_idx if signal else None, 1))
        return ev

    def dma(self, engname, out, in_, reads=(), writes=(), sem_buf=None, **kw):
        eng = self.engs[engname]
        reads = list(reads)
        writes = list(writes)
        assert not eng.pending
        waits = self._collect_waits(eng, reads, writes)
        sb = sem_buf if sem_buf is not None else (writes[0] if writes else reads[0])
        key = sb.sem_key
        if key not in self.dma_sems:
            self.dma_sems[key] = self._new_sem("d_" + key)
            self.dma_counts[key] = 0
        self.dma_counts[key] += 16
        ev = (self.dma_sems[key], self.dma_counts[key])
        for b in reads:
            b.reads.append(ev)
        for b in writes:
            b.last_write = ev
            b.reads = []
        eng.ops.append((waits, lambda e: e.dma_start(out=out, in_=in_, **kw), self.dma_sems[key], 16))
        return ev

    def barrier(self, engines=("pe", "act", "dve", "pool", "sp")):
        targets = []
        for e in self.engs.values():
            assert not e.pending, e.name
            if e.count:
                targets.append((e.sem_idx, e.count))
        for key, idx in self.dma_sems.items():
            targets.append((idx, self.dma_counts[key]))
        for n in engines:
            eng = self.engs[n]
            waits = []
            for s, v in targets:
                if s == eng.sem_idx:
                    continue
                if eng.waited.get(s, 0) >= v:
                    continue
                eng.waited[s] = v
                waits.append((s, v))
            if waits:
                eng.ops.append((waits, None, None, 0))

    def emit(self, enter):
        nc = self.nc
        sems = [enter(nc.semaphore(n)) for n in self.sem_names]
        block = enter(nc.Block())

        def run(eng):
            def body(e):
                for waits, fn, sig, inc in eng.ops:
                    for s, v in waits:
                        e.wait_ge(sems[s], v)
                    if fn is not None:
                        ins = fn(e)
                        if sig is not None:
                            ins.then_inc(sems[sig], inc)
            return body
        block.tensor(run(self.engs["pe"]))
        block.scalar(run(self.engs["act"]))
        block.vector(run(self.engs["dve"]))
        block.gpsimd(run(self.engs["pool"]))
        block.sync(run(self.engs["sp"]))


class Recorder:
    def __init__(self):
        self.calls = []

    def op(self, *a, **k):
        self.calls.append(("op", a, k))

    def dma(self, *a, **k):
        self.calls.append(("dma", a, k))

    def barrier(self):
        pass


class Arena:
    def __init__(self, ap, nbytes):
        self.ap = ap
        self.nbytes = nbytes
        self.off = 0

    def reset(self):
        self.off = 0

    def alloc(self, shape, dtype):
        n = 1
        for s in shape:
            n *= s
        esz = 4 if dtype == F32 else 2
        self.off = (self.off + 63) // 64 * 64
        a = self.off // 2
        nb = n * esz
        assert self.off + nb <= self.nbytes, ("arena overflow", self.off, nb, self.nbytes)
        self.peak = max(getattr(self, "peak", 0), self.off + nb)
        v = self.ap[:, a:a + nb // 2]
        self.off += nb
        if dtype == F32:
            v = v.bitcast(F32)
        if len(shape) == 2:
            v = v.rearrange("p (a b) -> p a b", a=shape[0])
        elif len(shape) == 3:
            v = v.rearrange("p (a b c) -> p a b c", a=shape[0], b=shape[1])
        elif len(shape) == 4:
            v = v.rearrange("p (a b c d) -> p a b c d", a=shape[0], b=shape[1], c=shape[2])
        return v


def build_program(S, L, dbg=None):
    dbg = dbg or set()
    NT = S // 512
    NB = S // 128
    NCH = S // LCH
    nc = bass.Bass("TRN2", target_bir_lowering=False)

    def din(name, shape):
        return nc.dram_tensor(name, list(shape), F32, kind="ExternalInput").ap()

    x_in = din("x", [SPC * S, D])
    norm_mix = din("norm_mix", [L, D])
    w_in = din("w_in", [L, D, NIN])
    b_forget = din("b_forget", [L, H])
    lam_re = din("ssm_lambda_re", [L, G * PS])
    lam_im = din("ssm_lambda_im", [L, G * PS])
    log_dt = din("ssm_log_dt", [L, G])
    b_re = din("ssm_b_re", [L, G * PS * CG])
    b_im = din("ssm_b_im", [L, G * PS * CG])
    c_re = din("ssm_c_re", [L, G * CG * PS])
    c_im = din("ssm_c_im", [L, G * CG * PS])
    ssm_d = din("ssm_d", [L, SW])
    w_glu = din("w_glu", [L, SW, SW])
    b_glu = din("b_glu", [L, SW])
    w_ba = din("w_branch_a", [L, AW, D])
    w_bb = din("w_branch_b", [L, SW, D])
    w_out = din("w_out", [L, D, D])
    norm_mlp = din("norm_mlp", [L, D])
    w_up = din("w_mlp_up", [L, D, DFF])
    w_down = din("w_mlp_down", [L, DFF, D])
    norm_final = din("norm_final", [1, D])
    out_ap = nc.dram_tensor("out", [SPC * S, D], F32, kind="ExternalOutput").ap()

    def scratch(name, shape, dtype):
        kind = "ExternalOutput" if name in dbg else "Internal"
        return nc.dram_tensor(name, list(shape), dtype, kind=kind).ap()

    XT = scratch("XT", [SPC, D, S], F32)
    QT = scratch("QT", [H, DH, S], BF16)
    KT = scratch("KT", [H, DH, S], BF16)
    VS = scratch("VS", [H, S, DH], BF16)
    CUMB = scratch("CUMB", [H, S], BF16)
    UT = scratch("UT", [SW, S], BF16)
    SGs = scratch("SG", [SPC, 2 * D, S], BF16)
    YAs = scratch("YA", [SPC, H, DH, S], BF16)
    YBs = scratch("YB", [SPC, SW, S], BF16)
    AM = scratch("AM", [L, 128, 4 * 2 * LCH * 128], BF16)
    CM = scratch("CM", [L, 128, 16 * 2 * (LCH + 1) * 32], BF16)
    KM = scratch("KM", [L, 128, 4 * LCH * 128], BF16)
    ET = scratch("ET", [L, 128, 3 * 16 * (S // LCH)], F32)

    with ExitStack() as st:
        E = st.enter_context
        P = Prog(nc)
        ARENA_BYTES = 196 * 1024
        arena_t = E(nc.sbuf_tensor("arena", [128, ARENA_BYTES // 2], BF16))
        arena = Arena(arena_t[:], ARENA_BYTES)
        ident_f = E(nc.sbuf_tensor("ident_f", [128, 128], F32))
        ident_b = E(nc.sbuf_tensor("ident_b", [128, 128], BF16))
        ones_b = E(nc.sbuf_tensor("ones_b", [128, 128], BF16))
        ones_f = E(nc.sbuf_tensor("ones_f", [128, 512], F32))
        masks = E(nc.sbuf_tensor("masks", [128, 4, 512], BF16))
        gmix = E(nc.sbuf_tensor("gmix", [128, L * 8], F32))
        gmlp = E(nc.sbuf_tensor("gmlp", [128, L * 8], F32))
        gfin = E(nc.sbuf_tensor("gfin", [128, 8], F32))
        bglu = E(nc.sbuf_tensor("bglu", [128, L * 4], F32))
        dsk = E(nc.sbuf_tensor("dsk", [128, L * 4], F32))
        nbf = E(nc.sbuf_tensor("nbf", [8, L], F32))
        nck = E(nc.sbuf_tensor("nck", [128, NB, H], F32))
        carry = E(nc.sbuf_tensor("carry", [8, 2], F32))
        epsc = E(nc.sbuf_tensor("epsc", [128, 1], F32))
        psum = [E(nc.psum_tensor("ps%d" % i, [128, 512], F32)) for i in range(8)]
        PB = [Buf("ps%d" % i) for i in range(8)]
        B_const = Buf("const")
        B_nck = Buf("nck")
        B_carry = Buf("carry")

        def setup_consts():
            P.op("pool", lambda e: e.memset(ident_f[:], 0.0), writes=[B_const])
            P.op("pool", lambda e: e.affine_select(out=ident_f[:], in_=ident_f[:], pattern=[[-1, 128]],
                                                   compare_op=ALU.not_equal, fill=1.0, base=0,
                                                   channel_multiplier=1), writes=[B_const])
            P.op("pool", lambda e: e.tensor_copy(out=ident_b[:], in_=ident_f[:]), reads=[B_const], writes=[B_const])
            P.op("pool", lambda e: e.memset(ones_b[:], 1.0), writes=[B_const])
            P.op("pool", lambda e: e.memset(ones_f[:], 1.0), writes=[B_const])
            P.op("pool", lambda e: e.memset(masks[:], 0.0), writes=[B_const])
            for j in range(4):
                P.op("pool", lambda e, j=j: e.affine_select(out=masks[:, j, :], in_=masks[:, j, :],
                                                            pattern=[[1, 512]], compare_op=ALU.is_ge,
                                                            fill=-30000.0, base=-128 * j,
                                                            channel_multiplier=-1), writes=[B_const])
            P.op("pool", lambda e: e.memset(carry[:], 0.0), writes=[B_carry])
            P.op("pool", lambda e: e.memset(epsc[:], EPS), writes=[B_const])
            arena.reset()
            stage = arena.alloc([128], F32)
            B_stage = Buf("cstage", "ld0")

            def load_cols(src2d, R, dst, scale=None):
                P.dma("sp", stage[0:R, :], src2d, writes=[B_stage])
                P.op("pe", lambda e: e.matmul(psum[0][:, 0:R], stage[0:R, :], ident_f[0:R, 0:R],
                                              start=True, stop=True),
                     reads=[B_stage, B_const], writes=[PB[0]])
                if scale is None:
                    P.op("dve", lambda e: e.tensor_copy(out=dst, in_=psum[0][:, 0:R]), reads=[PB[0]], writes=[B_const])
                else:
                    P.op("dve", lambda e: e.tensor_scalar(out=dst, in0=psum[0][:, 0:R], scalar1=scale, scalar2=None,
                                                          op0=ALU.mult), reads=[PB[0]], writes=[B_const])
            load_cols(norm_mix.rearrange("l (c p) -> (l c) p", p=128), L * 8, gmix[:, :])
            load_cols(norm_mlp.rearrange("l (c p) -> (l c) p", p=128), L * 8, gmlp[:, :])
            load_cols(norm_final.rearrange("l (c p) -> (l c) p", p=128), 8, gfin[:, :])
            load_cols(b_glu.rearrange("l (c p) -> (l c) p", p=128), L * 4, bglu[:, :])
            load_cols(ssm_d.rearrange("l (c p) -> (l c) p", p=128), L * 4, dsk[:, :])
            P.dma("sp", stage[0:L, 0:8], b_forget, writes=[B_stage])
            P.op("pe", lambda e: e.matmul(psum[0][0:8, 0:L], stage[0:L, 0:8], ident_f[0:L, 0:L], start=True, stop=True),
                 reads=[B_stage, B_const], writes=[PB[0]])
            P.op("dve", lambda e: e.tensor_scalar(out=nbf[:, :], in0=psum[0][0:8, 0:L], scalar1=-1.0, scalar2=None,
                                                  op0=ALU.mult), reads=[PB[0]], writes=[B_const])
            P.barrier()

        def phase_in():
            arena.reset()
            xs = [arena.alloc([D], F32) for _ in range(2)]
            xsB = [Buf("pin_x%d" % i, "ld%d" % i) for i in range(2)]
            ot = [arena.alloc([8, 128], F32) for _ in range(2)]
            otB = [Buf("pin_o%d" % i, "st%d" % i) for i in range(2)]
            nblk = SPC * NB
            P.dma("sp", xs[0], x_in[0:128, :], writes=[xsB[0]])
            for i in range(nblk):
                s, tb = divmod(i, NB)
                if i + 1 < nblk:
                    P.dma("sp", xs[(i + 1) % 2], x_in[(i + 1) * 128:(i + 2) * 128, :], writes=[xsB[(i + 1) % 2]])
                xa = xs[i % 2]
                o = ot[i % 2]
                for half in range(2):
                    pb = (2 * i + half) % 4
                    for c4 in range(4):
                        c = half * 4 + c4
                        P.op("pe", lambda e, pb=pb, c=c, c4=c4, xa=xa: e.matmul(
                            psum[pb][:, c4 * 128:(c4 + 1) * 128], xa[:, c * 128:(c + 1) * 128], ident_f[:, :],
                            start=True, stop=True),
                            reads=[xsB[i % 2], B_const], writes=[PB[pb]], signal=(c4 == 3))
                    eng = "act" if half == 0 else "dve"
                    if eng == "act":
                        P.op("act", lambda e, pb=pb, o=o, half=half: e.activation(
                            out=o[:, half * 4:(half + 1) * 4, :],
                            in_=psum[pb][:, :].rearrange("p (a b) -> p a b", a=4), func=AF.Copy),
                            reads=[PB[pb]], writes=[otB[i % 2]])
                    else:
                        P.op("dve", lambda e, pb=pb, o=o, half=half: e.tensor_copy(
                            out=o[:, half * 4:(half + 1) * 4, :],
                            in_=psum[pb][:, :].rearrange("p (a b) -> p a b", a=4)),
                            reads=[PB[pb]], writes=[otB[i % 2]])
                P.dma("sp", XT[s].rearrange("(c p) t -> p c t", p=128)[:, :, tb * 128:(tb + 1) * 128], o,
                      reads=[otB[i % 2]], sem_buf=otB[i % 2])
            P.barrier()

        def load_weight_cast(dst, src, B, nsplit=1):
            P.dma("pool", dst, src, writes=[B])

        def rms_stats(xT, xB, sq, sqB, rstd, rstdB, pbank):
            P.op("act", lambda e: e.activation(out=sq, in_=xT, func=AF.Square), reads=[xB], writes=[sqB])
            for c in range(8):
                P.op("pe", lambda e, c=c: e.matmul(psum[pbank][:, :], ones_b[:, :], sq[:, c, :],
                                                   start=(c == 0), stop=(c == 7)),
                     reads=[sqB, B_const], writes=[PB[pbank]], signal=(c == 7))
            P.op("act", lambda e: e.activation(out=rstd, in_=psum[pbank][:, :], func=AF.Ln, bias=epsc[:, 0:1],
                                               scale=1.0 / D), reads=[PB[pbank], B_const], writes=[rstdB])
            P.op("act", lambda e: e.activation(out=rstd, in_=rstd, func=AF.Exp, scale=-0.5), reads=[rstdB], writes=[rstdB])

        def phase1(l, s):
            arena.reset()
            WIN = arena.alloc([8, NIN], BF16)
            B_w = [Buf("p1_w%d" % c, "w%d" % c) for c in range(8)]
            xT = [arena.alloc([8, 512], F32) for _ in range(2)]
            xB = [Buf("p1_x%d" % i, "ld%d" % i) for i in range(2)]
            sq = arena.alloc([8, 512], BF16)
            sqB = Buf("p1_sq")
            rstd2 = [arena.alloc([512], F32) for _ in range(2)]
            rstd2B = [Buf("p1_rstd%d" % i) for i in range(2)]
            hT2 = [arena.alloc([8, 512], BF16) for _ in range(2)]
            hB2 = [Buf("p1_h%d" % i) for i in range(2)]
            QS = [arena.alloc([8, 512], BF16) for _ in range(1)]
            QSB = [Buf("p1_qs%d" % i, "st%d" % i) for i in range(1)]
            KS = [arena.alloc([8, 512], BF16) for _ in range(1)]
            KSB = [Buf("p1_ks%d" % i, "st%d" % (2 + i)) for i in range(1)]
            VSt = [arena.alloc([4, 512], BF16) for _ in range(1)]
            VSB = [Buf("p1_vs%d" % i, "st%d" % (4 + i)) for i in range(1)]
            US = [arena.alloc([4, 512], BF16) for _ in range(1)]
            USB = [Buf("p1_us%d" % i, "st%d" % (6 + i)) for i in range(1)]
            GS = [arena.alloc([16, 512], BF16) for _ in range(1)]
            GSB = [Buf("p1_gs%d" % i, "st%d" % (8 + i)) for i in range(1)]
            fz = arena.alloc([512], F32)
            fzB = Buf("p1_fz")
            ncum = [arena.alloc([512], F32) for _ in range(2)]
            ncumB = [Buf("p1_nc%d" % i) for i in range(2)]
            cb = [arena.alloc([512], BF16) for _ in range(2)]
            cbB = [Buf("p1_cb%d" % i, "st%d" % (10 + i)) for i in range(2)]

            wsrc = w_in[l].rearrange("(c p) n -> p c n", p=128)
            for c in range(8):
                P.dma("pool", WIN[:, c, :], wsrc[:, c, :], writes=[B_w[c]])
            xsrc = XT[s].rearrange("(c p) t -> p c t", p=128)
            P.dma("sp", xT[0], xsrc[:, :, 0:512], writes=[xB[0]])
            if NT > 1:
                P.dma("sp", xT[1], xsrc[:, :, 512:1024], writes=[xB[1]])
            pbi = [0]

            def nextbank():
                pbi[0] = (pbi[0] + 1) % 8
                return pbi[0]

            def prologue(t):
                sl = t % 2
                pb = nextbank()
                rms_stats(xT[sl], xB[sl], sq, sqB, rstd2[sl], rstd2B[sl], pb)
                for c in range(8):
                    P.op("dve", lambda e, c=c, sl=sl: e.scalar_tensor_tensor(
                        out=hT2[sl][:, c, :], in0=xT[sl][:, c, :], scalar=gmix[:, l * 8 + c:l * 8 + c + 1], in1=rstd2[sl],
                        op0=ALU.mult, op1=ALU.mult), reads=[xB[sl], rstd2B[sl], B_const], writes=[hB2[sl]])

            prologue(0)
            for t in range(NT):
                sl = t % 2
                hT = hT2[sl]
                hB = hB2[sl]
                if t + 1 < NT:
                    prologue(t + 1)
                if t + 2 < NT:
                    P.dma("sp", xT[t % 2], xsrc[:, :, (t + 2) * 512:(t + 3) * 512], writes=[xB[t % 2]])

                def fm_group(n0, M, hT=hT, hB=hB):
                    pb = nextbank()
                    for c in range(8):
                        P.op("pe", lambda e, c=c, pb=pb: e.matmul(psum[pb][0:M, :], WIN[:, c, n0:n0 + M], hT[:, c, :],
                                                                  start=(c == 0), stop=(c == 7)),
                             reads=[B_w[c], hB], writes=[PB[pb]], signal=(c == 7))
                    return pb
                pb = fm_group(O_F, 8)
                P.op("act", lambda e, pb=pb: e.activation(out=fz[0:8, :], in_=psum[pb][0:8, :], func=AF.Exp,
                                                          bias=nbf[:, l:l + 1], scale=-1.0),
                     reads=[PB[pb], B_const], writes=[fzB])
                P.op("act", lambda e: e.activation(out=fz[0:8, :], in_=fz[0:8, :], func=AF.Ln, bias=1.0, scale=1.0),
                     reads=[fzB], writes=[fzB])
                nslot = t % 2
                init = 0.0 if t == 0 else ncum[1 - nslot][0:8, 511:512]
                P.op("dve", lambda e, nslot=nslot, init=init: e.tensor_tensor_scan(
                    out=ncum[nslot][0:8, :], data0=ones_f[0:8, :], data1=fz[0:8, :], initial=init,
                    op0=ALU.mult, op1=ALU.add), reads=[fzB, B_const, ncumB[1 - nslot]], writes=[ncumB[nslot]])
                P.op("dve", lambda e, nslot=nslot: e.tensor_scalar(out=cb[nslot][0:8, :], in0=ncum[nslot][0:8, :],
                                                                    scalar1=-1.0, scalar2=None, op0=ALU.mult),
                     reads=[ncumB[nslot]], writes=[cbB[nslot]])
                P.dma("sp", CUMB[:, t * 512:(t + 1) * 512], cb[nslot][0:8, :], reads=[cbB[nslot]], sem_buf=cbB[nslot])
                for j in range(4):
                    pb = fm_group(O_Q + j * 128, 128)
                    P.op("act", lambda e, pb=pb, j=j, sl=sl: e.activation(out=QS[0][0:64, 2 * j, :], in_=psum[pb][0:64, :],
                                                                          func=AF.Copy, scale=0.125),
                         reads=[PB[pb]], writes=[QSB[0]])
                    P.op("dve", lambda e, pb=pb, j=j, sl=sl: e.tensor_scalar(out=QS[0][0:64, 2 * j + 1, :],
                                                                             in0=psum[pb][64:128, :], scalar1=0.125,
                                                                             scalar2=None, op0=ALU.mult),
                         reads=[PB[pb]], writes=[QSB[0]])
                P.dma("sp", QT.rearrange("h d t -> d h t")[:, :, t * 512:(t + 1) * 512], QS[0][0:64, :, :],
                      reads=[QSB[0]], sem_buf=QSB[0])
                for j in range(4):
                    pb = fm_group(O_K + j * 128, 128)
                    P.op("act", lambda e, pb=pb, j=j, sl=sl: e.activation(out=KS[0][0:64, 2 * j, :], in_=psum[pb][0:64, :],
                                                                          func=AF.Copy),
                         reads=[PB[pb]], writes=[KSB[0]])
                    P.op("dve", lambda e, pb=pb, j=j, sl=sl: e.tensor_copy(out=KS[0][0:64, 2 * j + 1, :],
                                                                           in_=psum[pb][64:128, :]),
                         reads=[PB[pb]], writes=[KSB[0]])
                P.dma("sp", KT.rearrange("h d t -> d h t")[:, :, t * 512:(t + 1) * 512], KS[0][0:64, :, :],
                      reads=[KSB[0]], sem_buf=KSB[0])
                for tb in range(4):
                    pb = nextbank()
                    for c in range(8):
                        P.op("pe", lambda e, c=c, pb=pb, tb=tb, hT=hT: e.matmul(
                            psum[pb][:, :], hT[:, c, tb * 128:(tb + 1) * 128], WIN[:, c, O_V:O_V + 512],
                            start=(c == 0), stop=(c == 7)),
                            reads=[B_w[c], hB], writes=[PB[pb]], signal=(c == 7))
                    if tb % 2 == 0:
                        P.op("act", lambda e, pb=pb, tb=tb, sl=sl: e.activation(out=VSt[0][:, tb, :], in_=psum[pb][:, :],
                                                                                func=AF.Copy),
                             reads=[PB[pb]], writes=[VSB[0]])
                    else:
                        P.op("dve", lambda e, pb=pb, tb=tb, sl=sl: e.tensor_copy(out=VSt[0][:, tb, :], in_=psum[pb][:, :]),
                             reads=[PB[pb]], writes=[VSB[0]])
                for tb in range(4):
                    r0 = t * 512 + tb * 128
                    P.dma("sp", VS.rearrange("h t d -> t h d")[r0:r0 + 128, :, :],
                          VSt[0][:, tb, :].rearrange("p (h d) -> p h d", h=8),
                          reads=[VSB[0]], sem_buf=VSB[0])
                for j in range(4):
                    pb = fm_group(O_U + j * 128, 128)
                    P.op("dve", lambda e, pb=pb, j=j, sl=sl: e.tensor_copy(out=US[0][:, j, :], in_=psum[pb][:, :]),
                         reads=[PB[pb]], writes=[USB[0]])
                P.dma("sp", UT.rearrange("(c p) t -> p c t", p=128)[:, :, t * 512:(t + 1) * 512], US[0],
                      reads=[USB[0]], sem_buf=USB[0])
                for j in range(16):
                    pb = fm_group(O_GA + j * 128, 128)
                    P.op("act", lambda e, pb=pb, j=j, sl=sl: e.activation(out=GS[0][:, j, :], in_=psum[pb][:, :],
                                                                          func=AF.Sigmoid),
                         reads=[PB[pb]], writes=[GSB[0]])
                P.dma("sp", SGs[s].rearrange("(c p) t -> p c t", p=128)[:, :, t * 512:(t + 1) * 512], GS[0],
                      reads=[GSB[0]], sem_buf=GSB[0])
                pbt = nextbank()
                for tb in range(4):
                    P.op("pe", lambda e, tb=tb, nslot=nslot, pbt=pbt: e.matmul(
                        psum[pbt][:, tb * 8:(tb + 1) * 8], ncum[nslot][0:8, tb * 128:(tb + 1) * 128],
                        ident_f[0:8, 0:8], start=True, stop=True),
                        reads=[ncumB[nslot], B_const], writes=[PB[pbt]], signal=(tb == 3))
                P.op("dve", lambda e, pbt=pbt, t=t: e.tensor_copy(
                    out=nck[:, t * 4:(t + 1) * 4, :], in_=psum[pbt][:, 0:32].rearrange("p (a b) -> p a b", a=4)),
                    reads=[PB[pbt]], writes=[B_nck])
            P.barrier()

        def replay(calls, n):
            for _ in range(min(n, len(calls))):
                kind, a, k = calls.pop(0)
                if kind == "op":
                    P.op(*a, **k)
                else:
                    P.dma(*a, **k)

        def phase2(l, s, side_calls=None):
            arena.reset()
            KTh = [arena.alloc([S], BF16) for _ in range(2)]
            KB = [Buf("p2_k%d" % i, "ld%d" % i) for i in range(2)]
            Vh = [arena.alloc([NB, 128], BF16) for _ in range(2)]
            VB = [Buf("p2_v%d" % i, "ld%d" % (2 + i)) for i in range(2)]
            NQ = 3
            Qt = [arena.alloc([512], BF16) for _ in range(NQ)]
            QB = [Buf("p2_q%d" % i, "ld%d" % (4 + i)) for i in range(NQ)]
            NP = 4
            Pt = [arena.alloc([512], BF16) for _ in range(NP)]
            PtB = [Buf("p2_p%d" % i) for i in range(NP)]
            RC = arena.alloc([512], F32)
            RCB = Buf("p2_rc")
            YS = [arena.alloc([512], BF16) for _ in range(2)]
            YSB = [Buf("p2_y%d" % i, "st%d" % i) for i in range(2)]
            SBK = [0, 1, 2, 3]
            OBK = [4, 5]
            for i in range(2):
                P.op("pool", lambda e, i=i: e.memset(Vh[i][:, :, 64:128], 1.0), writes=[VB[i]])
                P.op("pool", lambda e, i=i: e.memset(KTh[i][64:65, :], 1.0), writes=[KB[i]])

            def load_head(h):
                i = h % 2
                P.dma("sp", KTh[i][0:64, :], KT[h], writes=[KB[i]])
                P.dma("sp", Vh[i][:, :, 0:64], VS[h].rearrange("(b p) d -> p b d", p=128), writes=[VB[i]])

            items = []
            for h in range(H):
                for qt in range(NT):
                    for kb in range(4 * qt + 4):
                        items.append((h, qt, kb))
            qslot = {}
            qcount = [0]

            def load_q(h, qt):
                i = qcount[0] % NQ
                qcount[0] += 1
                qslot[(h, qt)] = i
                P.dma("sp", Qt[i][0:64, :], QT[h][:, qt * 512:(qt + 1) * 512], writes=[QB[i]])
                P.dma("sp", Qt[i][64:65, :], CUMB[h:h + 1, qt * 512:(qt + 1) * 512], writes=[QB[i]])

            def emit_qk(idx):
                h, qt, kb = items[idx]
                if idx == 0:
                    load_head(0)
                    load_head(1)
                if kb == 0:
                    if (h, qt) not in qslot:
                        load_q(h, qt)
                    nh, nqt = (h, qt + 1) if qt + 1 < NT else (h + 1, 0)
                    if nh < H and (nh, nqt) not in qslot:
                        load_q(nh, nqt)
                qi = qslot[(h, qt)]
                sb = SBK[idx % 4]
                j = kb - 4 * qt
                q0 = 128 * j if j > 0 else 0
                hi = h % 2
                diag = j >= 0
                P.op("pe", lambda e: e.matmul(psum[sb][:, q0:512], KTh[hi][0:65, kb * 128:(kb + 1) * 128],
                                              Qt[qi][0:65, q0:512], start=True, stop=not diag),
                     reads=[KB[hi], QB[qi]], writes=[PB[sb]], signal=not diag)
                if diag:
                    P.op("pe", lambda e: e.matmul(psum[sb][:, q0:512], ident_b[:, :], masks[:, j, q0:512],
                                                  start=False, stop=True),
                         reads=[B_const], writes=[PB[sb]], signal=True)

            def emit_rest(idx):
                h, qt, kb = items[idx]
                sb = SBK[idx % 4]
                pi = idx % NP
                j = kb - 4 * qt
                q0 = 128 * j if j > 0 else 0
                hi = h % 2
                oi = (h * NT + qt) % 2
                ob = OBK[oi]
                last = (kb == 4 * qt + 3)
                P.op("act", lambda e: e.activation(out=Pt[pi][:, q0:512], in_=psum[sb][:, q0:512], func=AF.Exp,
                                                   bias=nck[:, kb, h:h + 1], scale=1.0),
                     reads=[PB[sb], B_nck], writes=[PtB[pi]])
                P.op("pe", lambda e: e.matmul(psum[ob][:, q0:512], Vh[hi][:, kb, :], Pt[pi][:, q0:512],
                                              start=(kb == 0), stop=last),
                     reads=[VB[hi], PtB[pi]], writes=[PB[ob]], signal=True)
                if last:
                    P.op("dve", lambda e: e.reciprocal(out=RC[64:128, :], in_=psum[ob][64:128, :]),
                         reads=[PB[ob]], writes=[RCB])
                    P.op("dve", lambda e: e.tensor_tensor(out=YS[oi][0:64, :], in0=psum[ob][0:64, :], in1=RC[64:128, :],
                                                          op=ALU.mult), reads=[PB[ob], RCB], writes=[YSB[oi]])
                    P.dma("sp", YAs[s][h][:, qt * 512:(qt + 1) * 512], YS[oi][0:64, :], reads=[YSB[oi]], sem_buf=YSB[oi])
                    if qt == NT - 1 and h + 2 < H:
                        load_head(h + 2)

            LOOK = 2
            n = len(items)
            budget = 0.0
            acc = [0.0]

            def call_cost(c):
                kind, a, k = c
                if kind == "dma":
                    return 0.3
                return 0.05 if a[0] == "pe" else 0.55
            if side_calls:
                side_calls = list(side_calls)
                budget = sum(call_cost(c) for c in side_calls) / (0.8 * n)
            for idx in range(min(LOOK, n)):
                emit_qk(idx)
            for idx in range(n):
                if idx + LOOK < n:
                    emit_qk(idx + LOOK)
                emit_rest(idx)
                if side_calls:
                    acc[0] += budget
                    while side_calls and acc[0] > 0:
                        acc[0] -= call_cost(side_calls[0])
                        replay(side_calls, 1)
            if side_calls:
                replay(side_calls, len(side_calls))
            P.barrier()

        def bc_last(ap2, n):
            return ap2.unsqueeze(2).to_broadcast([ap2.shape[0], ap2.shape[1], n])

        def ssm_setup(l, PP, arena, banks, overlapped):
            import math
            arena.reset()
            TB = Buf("ssm_tab")
            stage = arena.alloc([128], F32)
            B_stage = Buf("sstage", "sx0")

            def V(fn, eng="dve"):
                PP.op(eng, fn, reads=[TB], writes=[TB])

            def tt(out, a, b, op, eng="dve"):
                V(lambda e: e.tensor_tensor(out=out, in0=a, in1=b, op=op), eng)

            def load_cols(src2d, R, dst):
                PP.dma("sp", stage[0:R, :], src2d, writes=[B_stage])
                PP.op("pe", lambda e: e.matmul(psum[banks[0]][:, 0:R], stage[0:R, :], ident_f[0:R, 0:R], start=True, stop=True),
                     reads=[B_stage, B_const], writes=[PB[banks[0]]])
                PP.op("dve", lambda e: e.tensor_copy(out=dst, in_=psum[banks[0]][:, 0:R]), reads=[PB[banks[0]], TB], writes=[TB])

            def ctab(LRE, LIM, LDT, shp, npow):
                T = {}
                def new(nm):
                    T[nm] = arena.alloc(shp, F32)
                    return T[nm]
                dt = new("dt"); a = new("a"); th = new("th"); mag = new("mag"); t = new("t")
                sv = new("sv"); cv = new("cv"); lbr = new("lbr"); lbi = new("lbi"); nr = new("nr")
                den = new("den"); kr = new("kr"); ki = new("ki")
                V(lambda e: e.activation(out=dt, in_=LDT, func=AF.Exp), "act")
                tt(a, LRE, dt, ALU.mult)
                tt(th, LIM, dt, ALU.mult)
                V(lambda e: e.activation(out=mag, in_=a, func=AF.Exp), "act")
                for (xv, sh) in ((sv, math.pi), (cv, 1.5 * math.pi)):
                    V(lambda e, xv=xv, sh=sh: e.tensor_scalar(out=xv, in0=th, scalar1=sh, scalar2=None, op0=ALU.add))
                    for _ in range(5):
                        V(lambda e, xv=xv: e.tensor_scalar(out=t, in0=xv, scalar1=2 * math.pi, scalar2=2 * math.pi,
                                                           op0=ALU.is_ge, op1=ALU.mult))
                        tt(xv, xv, t, ALU.subtract)
                    V(lambda e, xv=xv: e.tensor_scalar(out=xv, in0=xv, scalar1=-math.pi, scalar2=None, op0=ALU.add))
                    V(lambda e, xv=xv: e.tensor_scalar(out=xv, in0=xv, scalar1=math.pi, scalar2=-math.pi, op0=ALU.min, op1=ALU.max))
                    V(lambda e, xv=xv: e.activation(out=xv, in_=xv, func=AF.Sin), "act")
                tt(lbr, mag, cv, ALU.mult)
                tt(lbi, mag, sv, ALU.mult)
                V(lambda e: e.tensor_scalar(out=nr, in0=lbr, scalar1=-1.0, scalar2=None, op0=ALU.add))
                tt(den, LRE, LRE, ALU.mult)
                tt(t, LIM, LIM, ALU.mult)
                tt(den, den, t, ALU.add)
                V(lambda e: e.reciprocal(out=den, in_=den))
                tt(kr, nr, LRE, ALU.mult)
                tt(t, lbi, LIM, ALU.mult)
                tt(kr, kr, t, ALU.add)
                tt(kr, kr, den, ALU.mult)
                tt(ki, lbi, LRE, ALU.mult)
                tt(t, nr, LIM, ALU.mult)
                tt(ki, ki, t, ALU.subtract)
                tt(ki, ki, den, ALU.mult)
                pwr = [new("pwr%d" % k) for k in range(npow + 1)]
                pwi = [new("pwi%d" % k) for k in range(npow + 1)]
                V(lambda e: e.memset(pwr[0], 1.0))
                V(lambda e: e.memset(pwi[0], 0.0))

                def cmul(outr, outi, ar, ai, br, bi):
                    tt(outr, ar, br, ALU.mult)
                    tt(t, ai, bi, ALU.mult)
                    tt(outr, outr, t, ALU.subtract)
                    tt(outi, ar, bi, ALU.mult)
                    tt(t, ai, br, ALU.mult)
                    tt(outi, outi, t, ALU.add)
                for k in range(1, npow + 1):
                    cmul(pwr[k], pwi[k], pwr[k - 1], pwi[k - 1], lbr, lbi)
                T["pwr"] = pwr; T["pwi"] = pwi; T["cmul"] = cmul
                return T

            LREp = arena.alloc([16], F32); LIMp = arena.alloc([16], F32); LDTp = arena.alloc([16], F32)
            load_cols(lam_re[l].rearrange("(a b) -> a b", b=128), 16, LREp)
            load_cols(lam_im[l].rearrange("(a b) -> a b", b=128), 16, LIMp)
            ld2 = arena.alloc([2], F32)
            B_ld2 = Buf("sld2", "sx1")
            PP.dma("sp", ld2[0:16, :], log_dt[l].rearrange("(a b) -> a b", b=2), writes=[B_ld2])
            PP.op("dve", lambda e: e.tensor_copy(out=stage[0:16, :].rearrange("p (a b) -> p a b", a=2),
                                                in_=bc_last(ld2[0:16, :], 64)), reads=[B_ld2, B_stage], writes=[B_stage])
            PP.op("pe", lambda e: e.matmul(psum[banks[0]][:, 0:16], stage[0:16, :], ident_f[0:16, 0:16], start=True, stop=True),
                 reads=[B_stage, B_const], writes=[PB[banks[0]]])
            PP.op("dve", lambda e: e.tensor_copy(out=LDTp, in_=psum[banks[0]][:, 0:16]), reads=[PB[banks[0]], TB], writes=[TB])
            Tp = ctab(LREp, LIMp, LDTp, [16], LCH)
            pwr, pwi, cmul = Tp["pwr"], Tp["pwi"], Tp["cmul"]
            Xre = arena.alloc([16, 32], F32); Xim = arena.alloc([16, 32], F32)
            V(lambda e: e.memset(Xre, 0.0), "pool")
            V(lambda e: e.memset(Xim, 0.0), "pool")
            for (X, src) in ((Xre, b_re), (Xim, b_im)):
                v = src[l].rearrange("(a m p c) -> m p a c", m=2, p=64, c=16)
                for m in range(2):
                    PP.dma("sp", X[m * 64:(m + 1) * 64, :, m * 16:(m + 1) * 16], v[m], reads=[TB], writes=[TB], sem_buf=B_stage)
            AX = arena.alloc([LCH, 2, 16, 32], BF16)
            Fr = arena.alloc([16], F32); Fi = arena.alloc([16], F32)
            t1 = arena.alloc([16, 32], F32); t2 = arena.alloc([16, 32], F32)
            for k in range(LCH):
                cmul(Fr, Fi, pwr[k], pwi[k], Tp["kr"], Tp["ki"])
                tt(t1, Xre, bc_last(Fr, 32), ALU.mult)
                tt(t2, Xim, bc_last(Fi, 32), ALU.mult, "pool")
                tt(AX[:, k, 0, :, :], t1, t2, ALU.subtract)
                tt(t1, Xim, bc_last(Fr, 32), ALU.mult)
                tt(t2, Xre, bc_last(Fi, 32), ALU.mult, "pool")
                tt(AX[:, k, 1, :, :], t1, t2, ALU.add)
            SA = [arena.alloc([LCH * 128], BF16) for _ in range(2)]
            SAB = [Buf("ssm_sa%d" % i, "sx%d" % (2 + i)) for i in range(2)]
            AMv = AM[l].rearrange("r (k x) -> r k x", x=LCH * 128)
            cnt = 0
            for pi in range(16):
                kc, q = divmod(pi, 4)
                for part in range(2):
                    sa = SA[cnt % 2]; saB = SAB[cnt % 2]; cnt += 1
                    for jb in range(4):
                        pb = banks[(cnt * 4 + jb) % len(banks)]
                        for jj in range(4):
                            j = jb * 4 + jj
                            PP.op("pe", lambda e, pb=pb, jj=jj, j=j, pi=pi, part=part: e.matmul(
                                psum[pb][0:32, jj * 128:(jj + 1) * 128], AX[:, LCH - 1 - j, part, pi, :], ident_b[:, :],
                                start=True, stop=True), reads=[TB, B_const], writes=[PB[pb]], signal=(jj == 3))
                        eng = "act" if (jb % 2 == 0 and not overlapped) else "dve"
                        if eng == "act":
                            PP.op("act", lambda e, pb=pb, jb=jb, sa=sa: e.activation(out=sa[0:32, jb * 512:(jb + 1) * 512],
                                                                                    in_=psum[pb][0:32, :], func=AF.Copy),
                                 reads=[PB[pb]], writes=[saB])
                        else:
                            PP.op("dve", lambda e, pb=pb, jb=jb, sa=sa: e.tensor_copy(out=sa[0:32, jb * 512:(jb + 1) * 512],
                                                                                     in_=psum[pb][0:32, :]),
                                 reads=[PB[pb]], writes=[saB])
                    PP.dma("sp", AMv[32 * q:32 * q + 32, kc * 2 + part, :], sa[0:32, :], reads=[saB], sem_buf=saB)
            rho = arena.alloc([16], F32); irho = arena.alloc([16], F32)
            Mr = arena.alloc([16], F32); Mi = arena.alloc([16], F32); Mt = arena.alloc([16], F32)
            V(lambda e: e.activation(out=rho, in_=Tp["a"], func=AF.Exp, scale=float(LCH)), "act")
            V(lambda e: e.reciprocal(out=irho, in_=rho))
            tt(Mr, pwr[LCH], irho, ALU.mult)
            tt(Mi, pwi[LCH], irho, ALU.mult)
            mark_tab = arena.off
            EC = arena.alloc([16, NCH], F32); ES = arena.alloc([16, NCH], F32); RH = arena.alloc([16, NCH], F32)
            u1 = arena.alloc([16, NCH // 2], F32); u2 = arena.alloc([16, NCH // 2], F32)
            V(lambda e: e.memset(EC[:, :, 0:1], 1.0))
            V(lambda e: e.memset(ES[:, :, 0:1], 0.0))
            nst = NCH.bit_length() - 1
            for k in range(nst):
                w = 1 << k
                if k > 0:
                    tt(Mt, Mr, Mi, ALU.mult)
                    tt(Mr, Mr, Mr, ALU.mult)
                    tt(Mi, Mi, Mi, ALU.mult)
                    tt(Mr, Mr, Mi, ALU.subtract)
                    V(lambda e: e.tensor_scalar(out=Mi, in0=Mt, scalar1=2.0, scalar2=None, op0=ALU.mult))
                tt(u1[:, :, 0:w], EC[:, :, 0:w], bc_last(Mr, w), ALU.mult)
                tt(u2[:, :, 0:w], ES[:, :, 0:w], bc_last(Mi, w), ALU.mult, "pool")
                tt(EC[:, :, w:2 * w], u1[:, :, 0:w], u2[:, :, 0:w], ALU.subtract)
                tt(u1[:, :, 0:w], EC[:, :, 0:w], bc_last(Mi, w), ALU.mult)
                tt(u2[:, :, 0:w], ES[:, :, 0:w], bc_last(Mr, w), ALU.mult, "pool")
                tt(ES[:, :, w:2 * w], u1[:, :, 0:w], u2[:, :, 0:w], ALU.add)
            V(lambda e: e.tensor_copy(out=RH, in_=bc_last(rho, NCH)))
            V(lambda e: e.memset(RH[:, :, 0:1], 0.0))
            ETv = ET[l].rearrange("r (k a n) -> r k a n", k=3, a=16)
            for i, tl in enumerate((EC, ES, RH)):
                PP.dma("sp", ETv[:, i, :, :], tl, reads=[TB], writes=[TB], sem_buf=B_stage)
            arena.off = mark_tab

            LREf = arena.alloc([4, 64], F32); LIMf = arena.alloc([4, 64], F32); LDTf = arena.alloc([4, 64], F32)
            LDTs = arena.alloc([4], F32)
            for gl in range(8):
                for (dst, src, wid) in ((LREf, lam_re, 64), (LIMf, lam_im, 64)):
                    sap = bass.AP(src.tensor, src[l].offset + gl * 64, [[0, 16], [512, 4], [1, 64]])
                    PP.dma("sp", dst[gl * 16:(gl + 1) * 16, :, :], sap, reads=[TB], writes=[TB], sem_buf=B_stage)
                sap = bass.AP(log_dt.tensor, log_dt[l].offset + gl, [[0, 16], [8, 4], [1, 1]])
                PP.dma("sp", LDTs[gl * 16:(gl + 1) * 16, :].unsqueeze(2), sap, reads=[TB], writes=[TB], sem_buf=B_stage,
                      allow_slow_non_contiguous=True)
            Cre = arena.alloc([4, 64], F32); Cim = arena.alloc([4, 64], F32)
            PP.dma("sp", Cre, c_re[l].rearrange("(k r p) -> r k p", k=4, p=64), reads=[TB], writes=[TB], sem_buf=B_stage)
            PP.dma("sp", Cim, c_im[l].rearrange("(k r p) -> r k p", k=4, p=64), reads=[TB], writes=[TB], sem_buf=B_stage)
            V(lambda e: e.tensor_copy(out=LDTf, in_=bc_last(LDTs, 64)))
            Tf = ctab(LREf, LIMf, LDTf, [4, 64], LCH)
            fr, fi = Tf["pwr"], Tf["pwi"]
            CL = arena.alloc([LCH + 1, 2, 4, 64], BF16)
            v1 = arena.alloc([4, 64], F32); v2 = arena.alloc([4, 64], F32)
            for k in range(LCH + 1):
                tt(v1, Cre, fr[k], ALU.mult)
                tt(v2, Cim, fi[k], ALU.mult, "pool")
                tt(CL[:, k, 0, :, :], v1, v2, ALU.subtract)
                tt(v1, Cre, fi[k], ALU.mult)
                tt(v2, Cim, fr[k], ALU.mult, "pool")
                V(lambda e, k=k: e.scalar_tensor_tensor(out=CL[:, k, 1, :, :], in0=v1, scalar=-1.0, in1=v2,
                                                         op0=ALU.mult, op1=ALU.subtract))
            zt = arena.alloc([4 * LCH * 128], BF16)
            V(lambda e: e.memset(zt, 0.0), "pool")
            PP.dma("sp", KM[l], zt, reads=[TB], writes=[TB], sem_buf=B_stage)
            SC = [[arena.alloc([LCH + 1, 32], BF16) for _ in range(2)] for _ in range(2)]
            SCB = [Buf("ssm_sc%d" % i, "sx%d" % (4 + i)) for i in range(2)]
            for i in range(2):
                for part in range(2):
                    PP.op("pool", lambda e, i=i, part=part: e.memset(SC[i][part], 0.0), writes=[SCB[i]])
            SK = [arena.alloc([LCH, 32], BF16) for _ in range(2)]
            SKB = [Buf("ssm_sk%d" % i, "sx%d" % (6 + i)) for i in range(2)]
            CMv = CM[l].rearrange("r (a b x) -> r a b x", a=16, b=2)
            KMv = KM[l].rearrange("r (k t c) -> r k t c", k=4, t=LCH)
            for pi in range(16):
                kc, q = divmod(pi, 4)
                sl = pi % 2
                for part in range(2):
                    pa = banks[(pi * 2 + part) * 2 % len(banks)]
                    pb2 = banks[((pi * 2 + part) * 2 + 1) % len(banks)]
                    for k in range(LCH + 1):
                        bank = pa if k < LCH else pb2
                        col = (k % LCH) * 32
                        for mh in range(2):
                            PP.op("pe", lambda e, bank=bank, col=col, mh=mh, k=k, part=part, kc=kc, q=q: e.matmul(
                                psum[bank][mh * 64:(mh + 1) * 64, col:col + 32],
                                CL[32 * q:32 * q + 32, k, part, kc, :],
                                ident_b[32 * q:32 * q + 32, 32 * q:32 * q + 32],
                                start=True, stop=True, tile_position=(32 * q, 64 * mh)),
                                reads=[TB, B_const], writes=[PB[bank]], signal=(mh == 1 and (k == LCH - 1 or k == LCH)))
                    sc = SC[sl][part]
                    for mh in range(2):
                        eng = "act" if (mh == 0 and not overlapped) else "dve"
                        src = psum[pa][mh * 64:(mh + 1) * 64, :].rearrange("p (k c) -> p k c", c=32)[:, :, mh * 16:(mh + 1) * 16]
                        dst = sc[mh * 64:(mh + 1) * 64, 0:LCH, mh * 16:(mh + 1) * 16]
                        src2 = psum[pb2][mh * 64:(mh + 1) * 64, mh * 16:(mh + 1) * 16]
                        dst2 = sc[mh * 64:(mh + 1) * 64, LCH, mh * 16:(mh + 1) * 16]
                        if eng == "act":
                            PP.op("act", lambda e, src=src, dst=dst: e.activation(out=dst, in_=src, func=AF.Copy),
                                 reads=[PB[pa]], writes=[SCB[sl]])
                            PP.op("act", lambda e, src2=src2, dst2=dst2: e.activation(out=dst2, in_=src2, func=AF.Copy),
                                 reads=[PB[pb2]], writes=[SCB[sl]])
                        else:
                            PP.op("dve", lambda e, src=src, dst=dst: e.tensor_copy(out=dst, in_=src),
                                 reads=[PB[pa]], writes=[SCB[sl]])
                            PP.op("dve", lambda e, src2=src2, dst2=dst2: e.tensor_copy(out=dst2, in_=src2),
                                 reads=[PB[pb2]], writes=[SCB[sl]])
                    PP.dma("sp", CMv[:, pi, part, :], sc.rearrange("p k c -> p (k c)"), reads=[SCB[sl]], sem_buf=SCB[sl])
                pk = banks[-1]
                for tau in range(LCH):
                    PP.op("pe", lambda e, tau=tau, pi=pi, sl=sl: e.matmul(psum[pk][0:32, tau * 32:(tau + 1) * 32],
                                                                         AX[:, 0, 0, pi, :], SC[sl][0][:, tau, :],
                                                                         start=True, stop=False),
                         reads=[TB, SCB[sl]], writes=[PB[pk]], signal=False)
                    PP.op("pe", lambda e, tau=tau, pi=pi, sl=sl: e.matmul(psum[pk][0:32, tau * 32:(tau + 1) * 32],
                                                                         AX[:, 0, 1, pi, :], SC[sl][1][:, tau, :],
                                                                         start=False, stop=True),
                         reads=[TB, SCB[sl]], writes=[PB[pk]], signal=(tau == LCH - 1))
                sk = SK[sl]
                PP.op("dve", lambda e, sk=sk: e.tensor_copy(out=sk[0:32, :, :],
                                                           in_=psum[pk][0:32, :].rearrange("p (t c) -> p t c", c=32)),
                     reads=[PB[pk]], writes=[SKB[sl]])
                PP.dma("sp", KMv[32 * q:32 * q + 32, kc, :, 32 * q:32 * q + 32], sk[0:32, :, :], reads=[SKB[sl], TB],
                      sem_buf=SKB[sl])
            PP.barrier()

        def phase3(l, s):
            arena.reset()
            U2 = arena.alloc([4, LCH, NCH], BF16)
            UB = Buf("p3_u2")
            SPr = [arena.alloc([16, NCH], BF16) for _ in range(2)]
            SPB = Buf("p3_sp")
            SL = [arena.alloc([16, NCH], F32) for _ in range(2)]
            SLB = [Buf("p3_sl%d" % i) for i in range(2)]
            mark = arena.off
            Uraw = arena.alloc([4, S], BF16)
            UrB = [Buf("p3_ur%d" % c, "ld%d" % (3 + c)) for c in range(4)]
            usrc = UT.rearrange("(c p) t -> p c t", p=128)
            for c in range(4):
                P.dma("sp", Uraw[:, c, :], usrc[:, c, :], writes=[UrB[c]])
            for c in range(4):
                src = Uraw[:, c, :].rearrange("p (n j) -> p j n", j=LCH)
                if c % 2 == 0:
                    P.op("dve", lambda e, c=c, src=src: e.tensor_copy(out=U2[:, c, :, :], in_=src), reads=[UrB[c]], writes=[UB])
                else:
                    P.op("pool", lambda e, c=c, src=src: e.tensor_copy(out=U2[:, c, :, :], in_=src), reads=[UrB[c]], writes=[UB])
            Amat = arena.alloc([4, 2, LCH, 128], BF16)
            AB = Buf("p3_am", "ld1")
            P.dma("sp", Amat.rearrange("p a b c d -> p (a b c d)"), AM[l], writes=[AB])
            cnt = 0
            for pi in range(16):
                kc, q = divmod(pi, 4)
                for part in range(2):
                    pb = cnt % 8
                    cnt += 1
                    for j in range(LCH):
                        P.op("pe", lambda e, pb=pb, j=j, kc=kc, q=q, part=part: e.matmul(
                            psum[pb][:, 0:NCH], Amat[32 * q:32 * q + 32, kc, part, j, :],
                            U2[32 * q:32 * q + 32, kc, j, :], start=(j == 0), stop=(j == LCH - 1),
                            tile_position=(32 * q, 0)),
                            reads=[AB, UB], writes=[PB[pb]], signal=(j == LCH - 1))
                    if part == 0:
                        P.op("act", lambda e, pb=pb, pi=pi: e.activation(out=SL[0][:, pi, :], in_=psum[pb][:, 0:NCH],
                                                                         func=AF.Copy), reads=[PB[pb]], writes=[SLB[0]])
                    else:
                        P.op("dve", lambda e, pb=pb, pi=pi: e.tensor_copy(out=SL[1][:, pi, :], in_=psum[pb][:, 0:NCH]),
                             reads=[PB[pb]], writes=[SLB[1]])
            P.barrier()
            arena.off = mark
            EC = arena.alloc([16, NCH], F32); ES = arena.alloc([16, NCH], F32); RH = arena.alloc([16, NCH], F32)
            EB = Buf("p3_et", "ld1")
            ETv = ET[l].rearrange("r (k a n) -> r k a n", k=3, a=16)
            for i, tl in enumerate((EC, ES, RH)):
                P.dma("sp", tl, ETv[:, i, :, :], writes=[EB])
            Wr = arena.alloc([16, NCH], F32); Wi = arena.alloc([16, NCH], F32)
            m1 = arena.alloc([16, NCH], F32); m2 = arena.alloc([16, NCH], F32)
            WB_ = Buf("p3_w")

            def tt(out, a, b, op, eng="dve", reads=(), writes=()):
                P.op(eng, lambda e: e.tensor_tensor(out=out, in0=a, in1=b, op=op), reads=list(reads), writes=list(writes))
            Bm1, Bm2, BWr, BWi = Buf("p3_m1"), Buf("p3_m2"), Buf("p3_wr"), Buf("p3_wi")
            tt(Wr, EC, SL[0], ALU.mult, "dve", [EB, SLB[0]], [BWr])
            tt(m1, ES, SL[1], ALU.mult, "pool", [EB, SLB[1]], [Bm1])
            tt(Wr, Wr, m1, ALU.add, "dve", [BWr, Bm1], [BWr])
            tt(Wi, EC, SL[1], ALU.mult, "dve", [EB, SLB[1]], [BWi])
            tt(m2, ES, SL[0], ALU.mult, "pool", [EB, SLB[0]], [Bm2])
            tt(Wi, Wi, m2, ALU.subtract, "dve", [BWi, Bm2], [BWi])
            fl = "p a n -> p (a n)"
            P.op("dve", lambda e: e.tensor_tensor_scan(out=SL[0].rearrange(fl), data0=RH.rearrange(fl), data1=Wr.rearrange(fl),
                                                       initial=0.0, op0=ALU.mult, op1=ALU.add),
                 reads=[EB, BWr, Bm2], writes=[SLB[0]])
            P.op("dve", lambda e: e.tensor_tensor_scan(out=SL[1].rearrange(fl), data0=RH.rearrange(fl), data1=Wi.rearrange(fl),
                                                       initial=0.0, op0=ALU.mult, op1=ALU.add),
                 reads=[EB, BWi, Bm1], writes=[SLB[1]])
            n1 = NCH - 1
            P.op("pool", lambda e: e.memset(SPr[0][:, :, 0:1], 0.0), writes=[SPB])
            P.op("pool", lambda e: e.memset(SPr[1][:, :, 0:1], 0.0), writes=[SPB])
            tt(Wr[:, :, 0:n1], EC[:, :, 0:n1], SL[0][:, :, 0:n1], ALU.mult, "dve", [EB, SLB[0]], [BWr])
            tt(m1[:, :, 0:n1], ES[:, :, 0:n1], SL[1][:, :, 0:n1], ALU.mult, "pool", [EB, SLB[1]], [Bm1])
            tt(SPr[0][:, :, 1:NCH], Wr[:, :, 0:n1], m1[:, :, 0:n1], ALU.subtract, "dve", [BWr, Bm1], [SPB])
            tt(Wi[:, :, 0:n1], EC[:, :, 0:n1], SL[1][:, :, 0:n1], ALU.mult, "dve", [EB, SLB[1]], [BWi])
            tt(m2[:, :, 0:n1], ES[:, :, 0:n1], SL[0][:, :, 0:n1], ALU.mult, "pool", [EB, SLB[0]], [Bm2])
            tt(SPr[1][:, :, 1:NCH], Wi[:, :, 0:n1], m2[:, :, 0:n1], ALU.add, "dve", [BWi, Bm2], [SPB])
            P.barrier()
            arena.off = mark
            Cmat = arena.alloc([16, 2, LCH + 1, 32], BF16)
            CB = Buf("p3_cm", "ld1")
            P.dma("sp", Cmat.rearrange("p a b c d -> p (a b c d)"), CM[l], writes=[CB])
            Kmat = arena.alloc([4, LCH, 128], BF16)
            KB_ = Buf("p3_km", "ld2")
            P.dma("sp", Kmat.rearrange("p a b c -> p (a b c)"), KM[l], writes=[KB_])
            for kc in range(4):
                P.op("dve", lambda e, kc=kc: e.scalar_tensor_tensor(out=Kmat[:, kc, 0, :], in0=ident_f[:, :],
                                                                    scalar=dsk[:, l * 4 + kc:l * 4 + kc + 1],
                                                                    in1=Kmat[:, kc, 0, :], op0=ALU.mult, op1=ALU.add),
                     reads=[KB_, B_const], writes=[KB_])
            YS = [arena.alloc([S], BF16) for _ in range(2)]
            YSB = [Buf("p3_ys%d" % i, "st%d" % i) for i in range(2)]
            g1 = [arena.alloc([NCH], F32) for _ in range(2)]
            g2 = [arena.alloc([NCH], F32) for _ in range(2)]
            g1B = [Buf("p3_g1%d" % i) for i in range(2)]
            g2B = [Buf("p3_g2%d" % i) for i in range(2)]
            cnt = 0
            for kc in range(4):
                ys = YS[kc % 2]; ysB = YSB[kc % 2]
                for j in range(LCH):
                    pb = cnt % 8
                    gi = cnt % 2
                    cnt += 1
                    for i in range(j + 1):
                        P.op("pe", lambda e, pb=pb, i=i, j=j, kc=kc: e.matmul(
                            psum[pb][:, 0:NCH], Kmat[:, kc, j - i, :], U2[:, kc, i, :], start=(i == 0), stop=False),
                            reads=[KB_, UB], writes=[PB[pb]], signal=False)
                    for q in range(4):
                        pi = kc * 4 + q
                        for part in range(2):
                            last = (q == 3 and part == 1)
                            P.op("pe", lambda e, pb=pb, q=q, pi=pi, part=part, j=j, last=last: e.matmul(
                                psum[pb][32 * q:32 * q + 32, 0:NCH], Cmat[:, pi, part, j + 1, :], SPr[part][:, pi, :],
                                start=False, stop=last, tile_position=(0, 32 * q)),
                                reads=[CB, SPB], writes=[PB[pb]], signal=last)
                    P.op("act", lambda e, pb=pb, gi=gi: e.activation(out=g1[gi], in_=psum[pb][:, 0:NCH], func=AF.Square),
                         reads=[PB[pb]], writes=[g1B[gi]])
                    P.op("dve", lambda e, gi=gi: e.tensor_scalar(out=g1[gi], in0=g1[gi], scalar1=0.044715, scalar2=1.0,
                                                                 op0=ALU.mult, op1=ALU.add), reads=[g1B[gi]], writes=[g1B[gi]])
                    P.op("dve", lambda e, pb=pb, gi=gi: e.tensor_tensor(out=g2[gi], in0=psum[pb][:, 0:NCH], in1=g1[gi],
                                                                        op=ALU.mult), reads=[PB[pb], g1B[gi]], writes=[g2B[gi]])
                    P.op("act", lambda e, gi=gi: e.activation(out=g2[gi], in_=g2[gi], func=AF.Sigmoid, scale=1.5957691216),
                         reads=[g2B[gi]], writes=[g2B[gi]])
                    P.op("dve", lambda e, pb=pb, gi=gi, j=j, ys=ys: e.tensor_tensor(out=ys[:, j:S:LCH], in0=psum[pb][:, 0:NCH],
                                                                                    in1=g2[gi], op=ALU.mult),
                         reads=[PB[pb], g2B[gi]], writes=[ysB])
                P.dma("sp", YBs[s][kc * 128:(kc + 1) * 128, :], ys, reads=[ysB], sem_buf=ysB)
            P.barrier()

        def phase4(l):
            arena.reset()
            WG = arena.alloc([4, 512], BF16)
            WA = arena.alloc([4, 1024], BF16)
            WB = arena.alloc([4, 1024], BF16)
            WO = arena.alloc([8, 1024], BF16)
            B_wg, B_wa, B_wb, B_wo = Buf("p4_wg", "w0"), Buf("p4_wa", "w1"), Buf("p4_wb", "w2"), Buf("p4_wo", "w3")
            P.dma("pool", WG, w_glu[l].rearrange("(c p) n -> p c n", p=128), writes=[B_wg])
            P.dma("pool", WA, w_ba[l].rearrange("(c p) n -> p c n", p=128), writes=[B_wa])
            P.dma("pool", WB, w_bb[l].rearrange("(c p) n -> p c n", p=128), writes=[B_wb])
            P.dma("pool", WO, w_out[l].rearrange("(c p) n -> p c n", p=128), writes=[B_wo])
            ya = [arena.alloc([4, 512], BF16) for _ in range(2)]
            yb = [arena.alloc([4, 512], BF16) for _ in range(2)]
            sg = [arena.alloc([16, 512], BF16) for _ in range(2)]
            xT = [arena.alloc([8, 512], F32) for _ in range(2)]
            yaB = [Buf("p4_ya%d" % i, "ld%d" % i) for i in range(2)]
            ybB = [Buf("p4_yb%d" % i, "ld%d" % (2 + i)) for i in range(2)]
            sgB = [Buf("p4_sg%d" % i, "ld%d" % (4 + i)) for i in range(2)]
            xB = [Buf("p4_x%d" % i, "ld%d" % (6 + i)) for i in range(2)]
            sgl = [arena.alloc([4, 512], BF16) for _ in range(2)]
            sglB = [Buf("p4_sgl%d" % i) for i in range(2)]
            yb2 = [arena.alloc([4, 512], BF16) for _ in range(2)]
            yb2B = [Buf("p4_yb2%d" % i) for i in range(2)]
            t1 = [arena.alloc([512], F32) for _ in range(2)]
            t1B = [Buf("p4_t1%d" % i) for i in range(2)]
            t2 = [arena.alloc([512], F32) for _ in range(2)]
            t2B = [Buf("p4_t2%d" % i) for i in range(2)]
            mixed = [arena.alloc([8, 512], BF16) for _ in range(2)]
            mixB = [Buf("p4_mix%d" % i) for i in range(2)]
            tiles = [(s, t) for s in range(SPC) for t in range(NT)]
            NTL = len(tiles)

            def load(i):
                s, t = tiles[i]
                k = i % 2
                ts = slice(t * 512, (t + 1) * 512)
                P.dma("sp", yb[k], YBs[s].rearrange("(c p) t -> p c t", p=128)[:, :, ts], writes=[ybB[k]])
                P.dma("sp", ya[k], YAs[s].rearrange("h d t -> (h d) t").rearrange("(c p) t -> p c t", p=128)[:, :, ts],
                      writes=[yaB[k]])
                P.dma("sp", sg[k], SGs[s].rearrange("(c p) t -> p c t", p=128)[:, :, ts], writes=[sgB[k]])
                P.dma("sp", xT[k], XT[s].rearrange("(c p) t -> p c t", p=128)[:, :, ts], writes=[xB[k]])
            pbi = [0]

            def nextbank():
                pbi[0] = (pbi[0] + 1) % 8
                return pbi[0]

            def glu(i):
                k = i % 2
                for n in range(4):
                    pb = nextbank()
                    for c in range(4):
                        P.op("pe", lambda e, pb=pb, c=c, n=n: e.matmul(psum[pb][:, :], WG[:, c, n * 128:(n + 1) * 128],
                                                                        yb[k][:, c, :], start=(c == 0), stop=(c == 3)),
                             reads=[B_wg, ybB[k]], writes=[PB[pb]], signal=(c == 3))
                    P.op("act", lambda e, pb=pb, n=n: e.activation(out=sgl[k][:, n, :], in_=psum[pb][:, :], func=AF.Sigmoid,
                                                                   bias=bglu[:, l * 4 + n:l * 4 + n + 1], scale=1.0),
                         reads=[PB[pb], B_const], writes=[sglB[k]])
                    P.op("pool", lambda e, n=n: e.tensor_tensor(out=yb2[k][:, n, :], in0=yb[k][:, n, :], in1=sgl[k][:, n, :],
                                                                op=ALU.mult), reads=[ybB[k], sglB[k]], writes=[yb2B[k]])

            def branches(i):
                k = i % 2
                for n in range(8):
                    pa = nextbank()
                    for c in range(4):
                        P.op("pe", lambda e, pa=pa, c=c, n=n: e.matmul(psum[pa][:, :], WA[:, c, n * 128:(n + 1) * 128],
                                                                        ya[k][:, c, :], start=(c == 0), stop=(c == 3)),
                             reads=[B_wa, yaB[k]], writes=[PB[pa]], signal=(c == 3))
                    pb = nextbank()
                    for c in range(4):
                        P.op("pe", lambda e, pb=pb, c=c, n=n: e.matmul(psum[pb][:, :], WB[:, c, n * 128:(n + 1) * 128],
                                                                        yb2[k][:, c, :], start=(c == 0), stop=(c == 3)),
                             reads=[B_wb, yb2B[k]], writes=[PB[pb]], signal=(c == 3))
                    j = n % 2
                    P.op("dve", lambda e, pa=pa, n=n, j=j: e.tensor_tensor(out=t1[j], in0=psum[pa][:, :], in1=sg[k][:, n, :],
                                                                          op=ALU.mult), reads=[PB[pa], sgB[k]], writes=[t1B[j]])
                    P.op("dve", lambda e, pb=pb, n=n, j=j: e.tensor_tensor(out=t2[j], in0=psum[pb][:, :], in1=sg[k][:, 8 + n, :],
                                                                          op=ALU.mult), reads=[PB[pb], sgB[k]], writes=[t2B[j]])
                    P.op("pool", lambda e, n=n, j=j: e.tensor_tensor(out=mixed[k][:, n, :], in0=t1[j], in1=t2[j], op=ALU.add),
                         reads=[t1B[j], t2B[j]], writes=[mixB[k]])

            def outproj(i):
                s, t = tiles[i]
                k = i % 2
                for n in range(8):
                    pb = nextbank()
                    for c in range(8):
                        P.op("pe", lambda e, pb=pb, c=c, n=n: e.matmul(psum[pb][:, :], WO[:, c, n * 128:(n + 1) * 128],
                                                                        mixed[k][:, c, :], start=(c == 0), stop=(c == 7)),
                             reads=[B_wo, mixB[k]], writes=[PB[pb]], signal=(c == 7))
                    P.op("dve", lambda e, pb=pb, n=n: e.tensor_tensor(out=xT[k][:, n, :], in0=psum[pb][:, :], in1=xT[k][:, n, :],
                                                                      op=ALU.add), reads=[PB[pb], xB[k]], writes=[xB[k]])
                P.dma("sp", XT[s].rearrange("(c p) t -> p c t", p=128)[:, :, t * 512:(t + 1) * 512], xT[k],
                      reads=[xB[k]], sem_buf=xB[k])

            load(0)
            glu(0)
            for i in range(NTL):
                if i + 1 < NTL:
                    load(i + 1)
                branches(i)
                if i + 1 < NTL:
                    glu(i + 1)
                outproj(i)
            P.barrier()

        def phase5(l):
            arena.reset()
            TM = 256
            WU = arena.alloc([8, DFF], BF16)
            WD = arena.alloc([32, D], BF16)
            B_wu = [Buf("p5_wu%d" % c, "w%d" % c) for c in range(8)]
            B_wd = [Buf("p5_wd%d" % c, "w%d" % (8 + c)) for c in range(4)]
            usrc = w_up[l].rearrange("(c p) n -> p c n", p=128)
            dsrc = w_down[l].rearrange("(c p) n -> p c n", p=128)
            for c in range(8):
                P.dma("pool", WU[:, c, :], usrc[:, c, :], writes=[B_wu[c]])
            for c in range(4):
                P.dma("pool", WD[:, c * 8:(c + 1) * 8, :], dsrc[:, c * 8:(c + 1) * 8, :], writes=[B_wd[c]])
            NX = 3
            xT = [arena.alloc([8, TM], F32) for _ in range(NX)]
            xB = [Buf("p5_x%d" % i, "ld%d" % i) for i in range(NX)]
            sq = arena.alloc([8, TM], BF16)
            sqB = Buf("p5_sq")
            rstd = [arena.alloc([TM], F32) for _ in range(2)]
            rstdB = [Buf("p5_rstd%d" % i) for i in range(2)]
            hT = [arena.alloc([8, TM], BF16) for _ in range(2)]
            hB = [Buf("p5_h%d" % i) for i in range(2)]
            rl = [arena.alloc([TM], BF16) for _ in range(2)]
            rlB = [Buf("p5_rl%d" % i) for i in range(2)]
            act = arena.alloc([32, TM], BF16)
            actB = [Buf("p5_act%d" % i) for i in range(32)]
            tiles = [(s, t) for s in range(SPC) for t in range(S // TM)]
            NTL = len(tiles)

            def load(i):
                s, t = tiles[i]
                P.dma("sp", xT[i % NX], XT[s].rearrange("(c p) t -> p c t", p=128)[:, :, t * TM:(t + 1) * TM],
                      writes=[xB[i % NX]])
            pbi = [0]

            def nextbank():
                pbi[0] = (pbi[0] + 1) % 8
                return pbi[0]

            def prologue(i):
                kx = i % NX
                k2 = i % 2
                pb = nextbank()
                P.op("act", lambda e: e.activation(out=sq, in_=xT[kx], func=AF.Square), reads=[xB[kx]], writes=[sqB])
                for c in range(8):
                    P.op("pe", lambda e, c=c, pb=pb: e.matmul(psum[pb][:, 0:TM], ones_b[:, :], sq[:, c, :],
                                                              start=(c == 0), stop=(c == 7)),
                         reads=[sqB, B_const], writes=[PB[pb]], signal=(c == 7))
                P.op("act", lambda e, pb=pb: e.activation(out=rstd[k2], in_=psum[pb][:, 0:TM], func=AF.Ln, bias=epsc[:, 0:1],
                                                          scale=1.0 / D), reads=[PB[pb], B_const], writes=[rstdB[k2]])
                P.op("act", lambda e: e.activation(out=rstd[k2], in_=rstd[k2], func=AF.Exp, scale=-0.5),
                     reads=[rstdB[k2]], writes=[rstdB[k2]])
                for c in range(8):
                    P.op("dve", lambda e, c=c: e.scalar_tensor_tensor(
                        out=hT[k2][:, c, :], in0=xT[kx][:, c, :], scalar=gmlp[:, l * 8 + c:l * 8 + c + 1], in1=rstd[k2],
                        op0=ALU.mult, op1=ALU.mult), reads=[xB[kx], rstdB[k2], B_const], writes=[hB[k2]])

            def body5(i):
                s, t = tiles[i]
                kx = i % NX
                k2 = i % 2
                for n in range(32):
                    pb = nextbank()
                    for c in range(8):
                        P.op("pe", lambda e, pb=pb, c=c, n=n: e.matmul(psum[pb][:, 0:TM], WU[:, c, n * 128:(n + 1) * 128],
                                                                        hT[k2][:, c, :], start=(c == 0), stop=(c == 7)),
                             reads=[B_wu[c], hB[k2]], writes=[PB[pb]], signal=(c == 7))
                    j = n % 2
                    P.op("act", lambda e, pb=pb, j=j: e.activation(out=rl[j], in_=psum[pb][:, 0:TM], func=AF.Relu),
                         reads=[PB[pb]], writes=[rlB[j]])
                    P.op("pool", lambda e, n=n, j=j: e.tensor_tensor(out=act[:, n, :], in0=rl[j], in1=rl[j], op=ALU.mult),
                         reads=[rlB[j]], writes=[actB[n]])
                for n in range(8):
                    pb = nextbank()
                    for c in range(32):
                        P.op("pe", lambda e, pb=pb, c=c, n=n: e.matmul(psum[pb][:, 0:TM], WD[:, c, n * 128:(n + 1) * 128],
                                                                        act[:, c, :], start=(c == 0), stop=(c == 31)),
                             reads=[B_wd[c // 8], actB[c]], writes=[PB[pb]], signal=(c == 31))
                    P.op("dve", lambda e, pb=pb, n=n: e.tensor_tensor(out=xT[kx][:, n, :], in0=psum[pb][:, 0:TM],
                                                                      in1=xT[kx][:, n, :], op=ALU.add),
                         reads=[PB[pb], xB[kx]], writes=[xB[kx]])
                P.dma("sp", XT[s].rearrange("(c p) t -> p c t", p=128)[:, :, t * TM:(t + 1) * TM], xT[kx],
                      reads=[xB[kx]], sem_buf=xB[kx])

            load(0)
            if NTL > 1:
                load(1)
            prologue(0)
            for i in range(NTL):
                if i + 2 < NTL:
                    load(i + 2)
                if i + 1 < NTL:
                    prologue(i + 1)
                body5(i)
            P.barrier()

        def phase_out():
            arena.reset()
            xT = [arena.alloc([8, 512], F32) for _ in range(2)]
            xB = [Buf("po_x%d" % i, "ld%d" % i) for i in range(2)]
            sq = arena.alloc([8, 512], BF16)
            sqB = Buf("po_sq")
            rstd = arena.alloc([512], F32)
            rstdB = Buf("po_rstd")
            yT = arena.alloc([8, 512], F32)
            yB = Buf("po_y")
            ot = [arena.alloc([D], F32) for _ in range(2)]
            otB = [Buf("po_o%d" % i, "st%d" % i) for i in range(2)]
            tiles = [(s, t) for s in range(SPC) for t in range(NT)]

            def load(i):
                s, t = tiles[i]
                P.dma("sp", xT[i % 2], XT[s].rearrange("(c p) t -> p c t", p=128)[:, :, t * 512:(t + 1) * 512],
                      writes=[xB[i % 2]])
            load(0)
            cntb = [0]

            def bodyo(i):
                s, t = tiles[i]
                k = i % 2
                if i + 1 < len(tiles):
                    load(i + 1)
                rms_stats(xT[k], xB[k], sq, sqB, rstd, rstdB, 0)
                for c in range(8):
                    P.op("dve", lambda e, c=c: e.scalar_tensor_tensor(
                        out=yT[:, c, :], in0=xT[k][:, c, :], scalar=gfin[:, c:c + 1], in1=rstd,
                        op0=ALU.mult, op1=ALU.mult), reads=[xB[k], rstdB, B_const], writes=[yB])
                for tb in range(4):
                    o = ot[cntb[0] % 2]
                    oB = otB[cntb[0] % 2]
                    cntb[0] += 1
                    for half in range(2):
                        pb = 1 + (2 * cntb[0] + half) % 4
                        for c4 in range(4):
                            c = half * 4 + c4
                            P.op("pe", lambda e, pb=pb, c=c, c4=c4, tb=tb: e.matmul(
                                psum[pb][:, c4 * 128:(c4 + 1) * 128], yT[:, c, tb * 128:(tb + 1) * 128], ident_f[:, :],
                                start=True, stop=True), reads=[yB, B_const], writes=[PB[pb]], signal=(c4 == 3))
                        if half == 0:
                            P.op("act", lambda e, pb=pb, o=o: e.activation(out=o[:, 0:512], in_=psum[pb][:, :], func=AF.Copy),
                                 reads=[PB[pb]], writes=[oB])
                        else:
                            P.op("dve", lambda e, pb=pb, o=o: e.tensor_copy(out=o[:, 512:1024], in_=psum[pb][:, :]),
                                 reads=[PB[pb]], writes=[oB])
                    r0 = s * S + t * 512 + tb * 128
                    P.dma("sp", out_ap[r0:r0 + 128, :], o, reads=[oB], sem_buf=oB)
            for i in range(len(tiles)):
                bodyo(i)
            P.barrier()

        setup_consts()
        phase_in()
        P2_BYTES = 48 * 1024
        arena_side = Arena(arena_t[:, P2_BYTES // 2:], ARENA_BYTES - P2_BYTES)
        ssm_setup(0, P, arena, [0, 1, 2, 3, 4, 5, 6, 7], False)
        for l in range(L):
            for s in range(SPC):
                phase1(l, s)
                side = None
                if s == SPC - 1 and l + 1 < L:
                    rec = Recorder()
                    arena_side.reset()
                    ssm_setup(l + 1, rec, arena_side, [6, 7], True)
                    side = rec.calls
                phase2(l, s, side)
                phase3(l, s)
            phase4(l)
            if "stop4" not in dbg:
                phase5(l)
        if "stop4" not in dbg:
            phase_out()
        P.emit(E)
    return nc


_FLAT = {"ssm_lambda_re": (G * PS,), "ssm_lambda_im": (G * PS,), "ssm_b_re": (G * PS * CG,), "ssm_b_im": (G * PS * CG,),
         "ssm_c_re": (G * CG * PS,), "ssm_c_im": (G * CG * PS,)}


def kernel(**inputs):
    x = np.asarray(inputs["x"])
    B, S, _ = x.shape
    L = np.asarray(inputs["w_in"]).shape[0]
    assert B == NCORES * SPC
    nc = build_program(S, L)
    shared = {}
    for k, v in inputs.items():
        if k == "x":
            continue
        v = np.ascontiguousarray(np.asarray(v, dtype=np.float32))
        if k == "norm_final":
            v = v.reshape(1, D)
        elif k in _FLAT:
            v = v.reshape((v.shape[0],) + _FLAT[k])
        shared[k] = v
    in_maps = []
    for c in range(NCORES):
        m = dict(shared)
        m["x"] = np.ascontiguousarray(x[SPC * c:SPC * (c + 1)].reshape(SPC * S, D).astype(np.float32))
        in_maps.append(m)
    res = run_bass_kernel_spmd(nc, in_maps, core_ids=list(range(NCORES)))
    outs = [np.asarray(r["out"]).reshape(SPC, S, D) for r in res.results]
    return np.concatenate(outs, axis=0).astype(np.float32)
```

```python
import numpy as np
from contextlib import ExitStack
import concourse.bass as bass
import concourse.mybir as mybir
from concourse.bass_utils import run_bass_kernel_spmd

F32 = mybir.dt.float32
BF16 = mybir.dt.bfloat16
AF = mybir.ActivationFunctionType
ALU = mybir.AluOpType

D = 1024
KC = 8
H = 8
DH = 64
AW = 512
SW = 512
G = 32
PS = 64
CG = 16
DFF = 4096
NIN = 4104
O_Q, O_K, O_V, O_F, O_U, O_GA, O_GB = 0, 512, 1024, 1536, 1544, 2056, 3080
LCH = 16
EPS = 1e-6
NCORES = 8
SPC = 2


class Buf:
    __slots__ = ("name", "last_write", "reads", "sem_key")

    def __init__(self, name, sem_key=None):
        self.name = name
        self.last_write = None
        self.reads = []
        self.sem_key = sem_key if sem_key is not None else name


class Eng:
    def __init__(self, name, sem_idx):
        self.name = name
        self.sem_idx = sem_idx
        self.count = 0
        self.ops = []
        self.waited = {}
        self.pending = []


class Prog:
    def __init__(self, nc):
        self.nc = nc
        self.sem_names = []
        self.engs = {}
        for n in ("pe", "act", "dve", "pool", "sp"):
            self.engs[n] = Eng(n, self._new_sem("e_" + n))
        self.dma_sems = {}
        self.dma_counts = {}
        self.same_engine_sync = {"pe": False, "act": True, "dve": True, "pool": True, "sp": False}

    def _new_sem(self, name):
        self.sem_names.append(name)
        return len(self.sem_names) - 1

    def _collect_waits(self, eng, reads, writes):
        waits = {}

        def add(ev):
            if ev is None:
                return
            s, v = ev
            if waits.get(s, 0) < v:
                waits[s] = v
        for b in reads:
            add(b.last_write)
        for b in writes:
            add(b.last_write)
            for r in b.reads:
                add(r)
        out = []
        for s, v in waits.items():
            if s == eng.sem_idx and not self.same_engine_sync[eng.name]:
                continue
            if eng.waited.get(s, 0) >= v:
                continue
            eng.waited[s] = v
            out.append((s, v))
        return out

    def op(self, engname, fn, reads=(), writes=(), signal=True):
        eng = self.engs[engname]
        reads = list(reads)
        writes = list(writes)
        waits = self._collect_waits(eng, reads, writes)
        if signal:
            eng.count += 1
            ev = (eng.sem_idx, eng.count)
            for (r, w) in eng.pending + [(reads, writes)]:
                for b in r:
                    b.reads.append(ev)
                for b in w:
                    b.last_write = ev
                    b.reads = []
            eng.pending = []
        else:
            ev = None
            eng.pending.append((reads, writes))
        eng.ops.append((waits, fn, eng.sem_idx if signal else None, 1))
        return ev

    def dma(self, engname, out, in_, reads=(), writes=(), sem_buf=None, **kw):
        eng = self.engs[engname]
        reads = list(reads)
        writes = list(writes)
        assert not eng.pending
        waits = self._collect_waits(eng, reads, writes)
        sb = sem_buf if sem_buf is not None else (writes[0] if writes else reads[0])
        key = sb.sem_key
        if key not in self.dma_sems:
            self.dma_sems[key] = self._new_sem("d_" + key)
            self.dma_counts[key] = 0
        self.dma_counts[key] += 16
        ev = (self.dma_sems[key], self.dma_counts[key])
        for b in reads:
            b.reads.append(ev)
        for b in writes:
            b.last_write = ev
            b.reads = []
        eng.ops.append((waits, lambda e: e.dma_start(out=out, in_=in_, **kw), self.dma_sems[key], 16))
        return ev

    def mark(self, name):
        pass

    def barrier(self, engines=("pe", "act", "dve", "pool", "sp")):
        targets = []
        for e in self.engs.values():
            assert not e.pending, e.name
            if e.count:
                targets.append((e.sem_idx, e.count))
        for key, idx in self.dma_sems.items():
            targets.append((idx, self.dma_counts[key]))
        for n in engines:
            eng = self.engs[n]
            waits = []
            for s, v in targets:
                if s == eng.sem_idx:
                    continue
                if eng.waited.get(s, 0) >= v:
                    continue
                eng.waited[s] = v
                waits.append((s, v))
            if waits:
                eng.ops.append((waits, None, None, 0))

    def emit(self, enter):
        nc = self.nc
        sems = [enter(nc.semaphore(n)) for n in self.sem_names]
        block = enter(nc.Block())

        def run(eng):
            def body(e):
                for waits, fn, sig, inc in eng.ops:
                    for s, v in waits:
                        e.wait_ge(sems[s], v)
                    if fn is not None:
                        ins = fn(e)
                        if sig is not None:
                            ins.then_inc(sems[sig], inc)
            return body
        block.tensor(run(self.engs["pe"]))
        block.scalar(run(self.engs["act"]))
        block.vector(run(self.engs["dve"]))
        block.gpsimd(run(self.engs["pool"]))
        block.sync(run(self.engs["sp"]))


class Recorder:
    def __init__(self):
        self.calls = []

    def op(self, *a, **k):
        self.calls.append(("op", a, k))

    def dma(self, *a, **k):
        self.calls.append(("dma", a, k))

    def barrier(self):
        pass

    def mark(self, name):
        self.calls.append(("mark", (name,), {}))


class Arena:
    def __init__(self, ap, nbytes):
        self.ap = ap
        self.nbytes = nbytes
        self.off = 0

    def reset(self):
        self.off = 0

    def alloc(self, shape, dtype):
        n = 1
        for s in shape:
            n *= s
        esz = 4 if dtype == F32 else 2
        self.off = (self.off + 63) // 64 * 64
        a = self.off // 2
        nb = n * esz
        assert self.off + nb <= self.nbytes, ("arena overflow", self.off, nb, self.nbytes)
        self.peak = max(getattr(self, "peak", 0), self.off + nb)
        v = self.ap[:, a:a + nb // 2]
        self.off += nb
        if dtype == F32:
            v = v.bitcast(F32)
        if len(shape) == 2:
            v = v.rearrange("p (a b) -> p a b", a=shape[0])
        elif len(shape) == 3:
            v = v.rearrange("p (a b c) -> p a b c", a=shape[0], b=shape[1])
        elif len(shape) == 4:
            v = v.rearrange("p (a b c d) -> p a b c d", a=shape[0], b=shape[1], c=shape[2])
        return v


def build_program(S, L, dbg=None):
    dbg = dbg or set()
    NT = S // 512
    NB = S // 128
    NCH = S // LCH
    nc = bass.Bass("TRN2", target_bir_lowering=False)

    def din(name, shape):
        return nc.dram_tensor(name, list(shape), F32, kind="ExternalInput").ap()

    x_in = din("x", [SPC * S, D])
    norm_mix = din("norm_mix", [L, D])
    w_in = din("w_in", [L, D, NIN])
    b_forget = din("b_forget", [L, H])
    lam_re = din("ssm_lambda_re", [L, G * PS])
    lam_im = din("ssm_lambda_im", [L, G * PS])
    log_dt = din("ssm_log_dt", [L, G])
    b_re = din("ssm_b_re", [L, G * PS * CG])
    b_im = din("ssm_b_im", [L, G * PS * CG])
    c_re = din("ssm_c_re", [L, G * CG * PS])
    c_im = din("ssm_c_im", [L, G * CG * PS])
    ssm_d = din("ssm_d", [L, SW])
    w_glu = din("w_glu", [L, SW, SW])
    b_glu = din("b_glu", [L, SW])
    w_ba = din("w_branch_a", [L, AW, D])
    w_bb = din("w_branch_b", [L, SW, D])
    w_out = din("w_out", [L, D, D])
    norm_mlp = din("norm_mlp", [L, D])
    w_up = din("w_mlp_up", [L, D, DFF])
    w_down = din("w_mlp_down", [L, DFF, D])
    norm_final = din("norm_final", [1, D])
    out_ap = nc.dram_tensor("out", [SPC * S, D], F32, kind="ExternalOutput").ap()

    def scratch(name, shape, dtype):
        kind = "ExternalOutput" if name in dbg else "Internal"
        return nc.dram_tensor(name, list(shape), dtype, kind=kind).ap()

    XT = scratch("XT", [SPC, D, S], F32)
    QT = scratch("QT", [H, DH, S], BF16)
    KT = scratch("KT", [H, DH, S], BF16)
    VS = scratch("VS", [H, S, DH], BF16)
    CUMB = scratch("CUMB", [H, S], BF16)
    UT = scratch("UT", [SW, S], BF16)
    SGs = scratch("SG", [SPC, 2 * D, S], BF16)
    YAs = scratch("YA", [SPC, H, DH, S], BF16)
    YBs = scratch("YB", [SPC, SW, S], BF16)
    AM = scratch("AM", [L, 128, 4 * 2 * LCH * 128], BF16)
    CM = scratch("CM", [L, 128, 16 * 2 * (LCH + 1) * 32], BF16)
    KM = scratch("KM", [L, 128, 4 * LCH * 128], BF16)
    ET = scratch("ET", [L, 128, 3 * 16 * (S // LCH)], F32)

    with ExitStack() as st:
        E = st.enter_context
        P = Prog(nc)
        ARENA_BYTES = 196 * 1024
        arena_t = E(nc.sbuf_tensor("arena", [128, ARENA_BYTES // 2], BF16))
        arena = Arena(arena_t[:], ARENA_BYTES)
        ident_f = E(nc.sbuf_tensor("ident_f", [128, 128], F32))
        ident_b = E(nc.sbuf_tensor("ident_b", [128, 128], BF16))
        ones_b = E(nc.sbuf_tensor("ones_b", [128, 128], BF16))
        ones_f = E(nc.sbuf_tensor("ones_f", [128, 512], F32))
        masks = E(nc.sbuf_tensor("masks", [128, 4, 512], BF16))
        gmix = E(nc.sbuf_tensor("gmix", [128, L * 8], F32))
        gmlp = E(nc.sbuf_tensor("gmlp", [128, L * 8], F32))
        gfin = E(nc.sbuf_tensor("gfin", [128, 8], F32))
        bglu = E(nc.sbuf_tensor("bglu", [128, L * 4], F32))
        dsk = E(nc.sbuf_tensor("dsk", [128, L * 4], F32))
        nbf = E(nc.sbuf_tensor("nbf", [8, L], F32))
        nck = E(nc.sbuf_tensor("nck", [128, NB, H], F32))
        carry = E(nc.sbuf_tensor("carry", [8, 2], F32))
        epsc = E(nc.sbuf_tensor("epsc", [128, 1], F32))
        psum = [E(nc.psum_tensor("ps%d" % i, [128, 512], F32)) for i in range(8)]
        PB = [Buf("ps%d" % i) for i in range(8)]
        B_const = Buf("const")
        B_nck = Buf("nck")
        B_carry = Buf("carry")

        def setup_consts():
            P.op("pool", lambda e: e.memset(ident_f[:], 0.0), writes=[B_const])
            P.op("pool", lambda e: e.affine_select(out=ident_f[:], in_=ident_f[:], pattern=[[-1, 128]],
                                                   compare_op=ALU.not_equal, fill=1.0, base=0,
                                                   channel_multiplier=1), writes=[B_const])
            P.op("pool", lambda e: e.tensor_copy(out=ident_b[:], in_=ident_f[:]), reads=[B_const], writes=[B_const])
            P.op("pool", lambda e: e.memset(ones_b[:], 1.0), writes=[B_const])
            P.op("pool", lambda e: e.memset(ones_f[:], 1.0), writes=[B_const])
            P.op("pool", lambda e: e.memset(masks[:], 0.0), writes=[B_const])
            for j in range(4):
                P.op("pool", lambda e, j=j: e.affine_select(out=masks[:, j, :], in_=masks[:, j, :],
                                                            pattern=[[1, 512]], compare_op=ALU.is_ge,
                                                            fill=-30000.0, base=-128 * j,
                                                            channel_multiplier=-1), writes=[B_const])
            P.op("pool", lambda e: e.memset(carry[:], 0.0), writes=[B_carry])
            P.op("pool", lambda e: e.memset(epsc[:], EPS), writes=[B_const])
            arena.reset()
            stage = arena.alloc([128], F32)
            B_stage = Buf("cstage", "ld0")

            def load_cols(src2d, R, dst, scale=None):
                P.dma("sp", stage[0:R, :], src2d, writes=[B_stage])
                P.op("pe", lambda e: e.matmul(psum[0][:, 0:R], stage[0:R, :], ident_f[0:R, 0:R],
                                              start=True, stop=True),
                     reads=[B_stage, B_const], writes=[PB[0]])
                if scale is None:
                    P.op("dve", lambda e: e.tensor_copy(out=dst, in_=psum[0][:, 0:R]), reads=[PB[0]], writes=[B_const])
                else:
                    P.op("dve", lambda e: e.tensor_scalar(out=dst, in0=psum[0][:, 0:R], scalar1=scale, scalar2=None,
                                                          op0=ALU.mult), reads=[PB[0]], writes=[B_const])
            load_cols(norm_mix.rearrange("l (c p) -> (l c) p", p=128), L * 8, gmix[:, :])
            load_cols(norm_mlp.rearrange("l (c p) -> (l c) p", p=128), L * 8, gmlp[:, :])
            load_cols(norm_final.rearrange("l (c p) -> (l c) p", p=128), 8, gfin[:, :])
            load_cols(b_glu.rearrange("l (c p) -> (l c) p", p=128), L * 4, bglu[:, :])
            load_cols(ssm_d.rearrange("l (c p) -> (l c) p", p=128), L * 4, dsk[:, :])
            P.dma("sp", stage[0:L, 0:8], b_forget, writes=[B_stage])
            P.op("pe", lambda e: e.matmul(psum[0][0:8, 0:L], stage[0:L, 0:8], ident_f[0:L, 0:L], start=True, stop=True),
                 reads=[B_stage, B_const], writes=[PB[0]])
            P.op("dve", lambda e: e.tensor_scalar(out=nbf[:, :], in0=psum[0][0:8, 0:L], scalar1=-1.0, scalar2=None,
                                                  op0=ALU.mult), reads=[PB[0]], writes=[B_const])
            P.barrier()

        def phase_in():
            arena.reset()
            xs = [arena.alloc([D], F32) for _ in range(2)]
            xsB = [Buf("pin_x%d" % i, "ld%d" % i) for i in range(2)]
            ot = [arena.alloc([8, 128], F32) for _ in range(2)]
            otB = [Buf("pin_o%d" % i, "st%d" % i) for i in range(2)]
            nblk = SPC * NB
            P.dma("sp", xs[0], x_in[0:128, :], writes=[xsB[0]])
            for i in range(nblk):
                s, tb = divmod(i, NB)
                if i + 1 < nblk:
                    P.dma("sp", xs[(i + 1) % 2], x_in[(i + 1) * 128:(i + 2) * 128, :], writes=[xsB[(i + 1) % 2]])
                xa = xs[i % 2]
                o = ot[i % 2]
                for half in range(2):
                    pb = (2 * i + half) % 4
                    for c4 in range(4):
                        c = half * 4 + c4
                        P.op("pe", lambda e, pb=pb, c=c, c4=c4, xa=xa: e.matmul(
                            psum[pb][:, c4 * 128:(c4 + 1) * 128], xa[:, c * 128:(c + 1) * 128], ident_f[:, :],
                            start=True, stop=True),
                            reads=[xsB[i % 2], B_const], writes=[PB[pb]], signal=(c4 == 3))
                    eng = "act" if half == 0 else "dve"
                    if eng == "act":
                        P.op("act", lambda e, pb=pb, o=o, half=half: e.activation(
                            out=o[:, half * 4:(half + 1) * 4, :],
                            in_=psum[pb][:, :].rearrange("p (a b) -> p a b", a=4), func=AF.Copy),
                            reads=[PB[pb]], writes=[otB[i % 2]])
                    else:
                        P.op("dve", lambda e, pb=pb, o=o, half=half: e.tensor_copy(
                            out=o[:, half * 4:(half + 1) * 4, :],
                            in_=psum[pb][:, :].rearrange("p (a b) -> p a b", a=4)),
                            reads=[PB[pb]], writes=[otB[i % 2]])
                P.dma("sp", XT[s].rearrange("(c p) t -> p c t", p=128)[:, :, tb * 128:(tb + 1) * 128], o,
                      reads=[otB[i % 2]], sem_buf=otB[i % 2])
            P.barrier()

        def load_weight_cast(dst, src, B, nsplit=1):
            P.dma("pool", dst, src, writes=[B])

        def rms_stats(xT, xB, sq, sqB, rstd, rstdB, pbank):
            P.op("act", lambda e: e.activation(out=sq, in_=xT, func=AF.Square), reads=[xB], writes=[sqB])
            for c in range(8):
                P.op("pe", lambda e, c=c: e.matmul(psum[pbank][:, :], ones_b[:, :], sq[:, c, :],
                                                   start=(c == 0), stop=(c == 7)),
                     reads=[sqB, B_const], writes=[PB[pbank]], signal=(c == 7))
            P.op("act", lambda e: e.activation(out=rstd, in_=psum[pbank][:, :], func=AF.Ln, bias=epsc[:, 0:1],
                                               scale=1.0 / D), reads=[PB[pbank], B_const], writes=[rstdB])
            P.op("act", lambda e: e.activation(out=rstd, in_=rstd, func=AF.Exp, scale=-0.5), reads=[rstdB], writes=[rstdB])

        def phase1(l, s):
            arena.reset()
            WIN = arena.alloc([8, NIN], BF16)
            B_w = [Buf("p1_w%d" % c, "w%d" % c) for c in range(8)]
            xT = [arena.alloc([8, 512], F32) for _ in range(2)]
            xB = [Buf("p1_x%d" % i, "ld%d" % i) for i in range(2)]
            sq = arena.alloc([8, 512], BF16)
            sqB = Buf("p1_sq")
            rstd2 = [arena.alloc([512], F32) for _ in range(2)]
            rstd2B = [Buf("p1_rstd%d" % i) for i in range(2)]
            hT2 = [arena.alloc([8, 512], BF16) for _ in range(2)]
            hB2 = [Buf("p1_h%d" % i) for i in range(2)]
            QS = [arena.alloc([8, 512], BF16) for _ in range(1)]
            QSB = [Buf("p1_qs%d" % i, "st%d" % i) for i in range(1)]
            KS = [arena.alloc([8, 512], BF16) for _ in range(1)]
            KSB = [Buf("p1_ks%d" % i, "st%d" % (2 + i)) for i in range(1)]
            VSt = [arena.alloc([4, 512], BF16) for _ in range(1)]
            VSB = [Buf("p1_vs%d" % i, "st%d" % (4 + i)) for i in range(1)]
            US = [arena.alloc([4, 512], BF16) for _ in range(1)]
            USB = [Buf("p1_us%d" % i, "st%d" % (6 + i)) for i in range(1)]
            GS = [arena.alloc([16, 512], BF16) for _ in range(1)]
            GSB = [Buf("p1_gs%d" % i, "st%d" % (8 + i)) for i in range(1)]
            fz = arena.alloc([512], F32)
            fzB = Buf("p1_fz")
            ncum = [arena.alloc([512], F32) for _ in range(2)]
            ncumB = [Buf("p1_nc%d" % i) for i in range(2)]
            cb = [arena.alloc([512], BF16) for _ in range(2)]
            cbB = [Buf("p1_cb%d" % i, "st%d" % (10 + i)) for i in range(2)]

            wsrc = w_in[l].rearrange("(c p) n -> p c n", p=128)
            for c in range(8):
                P.dma("pool", WIN[:, c, :], wsrc[:, c, :], writes=[B_w[c]])
            xsrc = XT[s].rearrange("(c p) t -> p c t", p=128)
            P.dma("sp", xT[0], xsrc[:, :, 0:512], writes=[xB[0]])
            if NT > 1:
                P.dma("sp", xT[1], xsrc[:, :, 512:1024], writes=[xB[1]])
            pbi = [0]

            def nextbank():
                pbi[0] = (pbi[0] + 1) % 8
                return pbi[0]

            def prologue(t):
                sl = t % 2
                pb = nextbank()
                rms_stats(xT[sl], xB[sl], sq, sqB, rstd2[sl], rstd2B[sl], pb)
                for c in range(8):
                    P.op("dve", lambda e, c=c, sl=sl: e.scalar_tensor_tensor(
                        out=hT2[sl][:, c, :], in0=xT[sl][:, c, :], scalar=gmix[:, l * 8 + c:l * 8 + c + 1], in1=rstd2[sl],
                        op0=ALU.mult, op1=ALU.mult), reads=[xB[sl], rstd2B[sl], B_const], writes=[hB2[sl]])

            prologue(0)
            for t in range(NT):
                sl = t % 2
                hT = hT2[sl]
                hB = hB2[sl]
                if t + 1 < NT:
                    prologue(t + 1)
                if t + 2 < NT:
                    P.dma("sp", xT[t % 2], xsrc[:, :, (t + 2) * 512:(t + 3) * 512], writes=[xB[t % 2]])

                def fm_group(n0, M, hT=hT, hB=hB):
                    pb = nextbank()
                    for c in range(8):
                        P.op("pe", lambda e, c=c, pb=pb: e.matmul(psum[pb][0:M, :], WIN[:, c, n0:n0 + M], hT[:, c, :],
                                                                  start=(c == 0), stop=(c == 7)),
                             reads=[B_w[c], hB], writes=[PB[pb]], signal=(c == 7))
                    return pb
                pb = fm_group(O_F, 8)
                P.op("act", lambda e, pb=pb: e.activation(out=fz[0:8, :], in_=psum[pb][0:8, :], func=AF.Exp,
                                                          bias=nbf[:, l:l + 1], scale=-1.0),
                     reads=[PB[pb], B_const], writes=[fzB])
                P.op("act", lambda e: e.activation(out=fz[0:8, :], in_=fz[0:8, :], func=AF.Ln, bias=1.0, scale=1.0),
                     reads=[fzB], writes=[fzB])
                nslot = t % 2
                init = 0.0 if t == 0 else ncum[1 - nslot][0:8, 511:512]
                P.op("dve", lambda e, nslot=nslot, init=init: e.tensor_tensor_scan(
                    out=ncum[nslot][0:8, :], data0=ones_f[0:8, :], data1=fz[0:8, :], initial=init,
                    op0=ALU.mult, op1=ALU.add), reads=[fzB, B_const, ncumB[1 - nslot]], writes=[ncumB[nslot]])
                P.op("dve", lambda e, nslot=nslot: e.tensor_scalar(out=cb[nslot][0:8, :], in0=ncum[nslot][0:8, :],
                                                                    scalar1=-1.0, scalar2=None, op0=ALU.mult),
                     reads=[ncumB[nslot]], writes=[cbB[nslot]])
                P.dma("sp", CUMB[:, t * 512:(t + 1) * 512], cb[nslot][0:8, :], reads=[cbB[nslot]], sem_buf=cbB[nslot])
                for j in range(4):
                    pb = fm_group(O_Q + j * 128, 128)
                    P.op("act", lambda e, pb=pb, j=j, sl=sl: e.activation(out=QS[0][0:64, 2 * j, :], in_=psum[pb][0:64, :],
                                                                          func=AF.Copy, scale=0.125),
                         reads=[PB[pb]], writes=[QSB[0]])
                    P.op("dve", lambda e, pb=pb, j=j, sl=sl: e.tensor_scalar(out=QS[0][0:64, 2 * j + 1, :],
                                                                             in0=psum[pb][64:128, :], scalar1=0.125,
                                                                             scalar2=None, op0=ALU.mult),
                         reads=[PB[pb]], writes=[QSB[0]])
                P.dma("sp", QT.rearrange("h d t -> d h t")[:, :, t * 512:(t + 1) * 512], QS[0][0:64, :, :],
                      reads=[QSB[0]], sem_buf=QSB[0])
                for j in range(4):
                    pb = fm_group(O_K + j * 128, 128)
                    P.op("act", lambda e, pb=pb, j=j, sl=sl: e.activation(out=KS[0][0:64, 2 * j, :], in_=psum[pb][0:64, :],
                                                                          func=AF.Copy),
                         reads=[PB[pb]], writes=[KSB[0]])
                    P.op("dve", lambda e, pb=pb, j=j, sl=sl: e.tensor_copy(out=KS[0][0:64, 2 * j + 1, :],
                                                                           in_=psum[pb][64:128, :]),
                         reads=[PB[pb]], writes=[KSB[0]])
                P.dma("sp", KT.rearrange("h d t -> d h t")[:, :, t * 512:(t + 1) * 512], KS[0][0:64, :, :],
                      reads=[KSB[0]], sem_buf=KSB[0])
                for tb in range(4):
                    pb = nextbank()
                    for c in range(8):
                        P.op("pe", lambda e, c=c, pb=pb, tb=tb, hT=hT: e.matmul(
                            psum[pb][:, :], hT[:, c, tb * 128:(tb + 1) * 128], WIN[:, c, O_V:O_V + 512],
                            start=(c == 0), stop=(c == 7)),
                            reads=[B_w[c], hB], writes=[PB[pb]], signal=(c == 7))
                    if tb % 2 == 0:
                        P.op("act", lambda e, pb=pb, tb=tb, sl=sl: e.activation(out=VSt[0][:, tb, :], in_=psum[pb][:, :],
                                                                                func=AF.Copy),
                             reads=[PB[pb]], writes=[VSB[0]])
                    else:
                        P.op("dve", lambda e, pb=pb, tb=tb, sl=sl: e.tensor_copy(out=VSt[0][:, tb, :], in_=psum[pb][:, :]),
                             reads=[PB[pb]], writes=[VSB[0]])
                for tb in range(4):
                    r0 = t * 512 + tb * 128
                    P.dma("sp", VS.rearrange("h t d -> t h d")[r0:r0 + 128, :, :],
                          VSt[0][:, tb, :].rearrange("p (h d) -> p h d", h=8),
                          reads=[VSB[0]], sem_buf=VSB[0])
                for j in range(4):
                    pb = fm_group(O_U + j * 128, 128)
                    P.op("dve", lambda e, pb=pb, j=j, sl=sl: e.tensor_copy(out=US[0][:, j, :], in_=psum[pb][:, :]),
                         reads=[PB[pb]], writes=[USB[0]])
                P.dma("sp", UT.rearrange("(c p) t -> p c t", p=128)[:, :, t * 512:(t + 1) * 512], US[0],
                      reads=[USB[0]], sem_buf=USB[0])
                for j in range(16):
                    pb = fm_group(O_GA + j * 128, 128)
                    P.op("act", lambda e, pb=pb, j=j, sl=sl: e.activation(out=GS[0][:, j, :], in_=psum[pb][:, :],
                                                                          func=AF.Sigmoid),
                         reads=[PB[pb]], writes=[GSB[0]])
                P.dma("sp", SGs[s].rearrange("(c p) t -> p c t", p=128)[:, :, t * 512:(t + 1) * 512], GS[0],
                      reads=[GSB[0]], sem_buf=GSB[0])
                pbt = nextbank()
                for tb in range(4):
                    P.op("pe", lambda e, tb=tb, nslot=nslot, pbt=pbt: e.matmul(
                        psum[pbt][:, tb * 8:(tb + 1) * 8], ncum[nslot][0:8, tb * 128:(tb + 1) * 128],
                        ident_f[0:8, 0:8], start=True, stop=True),
                        reads=[ncumB[nslot], B_const], writes=[PB[pbt]], signal=(tb == 3))
                P.op("dve", lambda e, pbt=pbt, t=t: e.tensor_copy(
                    out=nck[:, t * 4:(t + 1) * 4, :], in_=psum[pbt][:, 0:32].rearrange("p (a b) -> p a b", a=4)),
                    reads=[PB[pbt]], writes=[B_nck])
            P.barrier()

        def replay(calls, n):
            for _ in range(min(n, len(calls))):
                kind, a, k = calls.pop(0)
                if kind == "op":
                    P.op(*a, **k)
                else:
                    P.dma(*a, **k)

        def phase2(l, s, side_calls=None):
            arena.reset()
            KTh = [arena.alloc([S], BF16) for _ in range(2)]
            KB = [Buf("p2_k%d" % i, "ld%d" % i) for i in range(2)]
            Vh = [arena.alloc([NB, 128], BF16) for _ in range(2)]
            VB = [Buf("p2_v%d" % i, "ld%d" % (2 + i)) for i in range(2)]
            NQ = 3
            Qt = [arena.alloc([512], BF16) for _ in range(NQ)]
            QB = [Buf("p2_q%d" % i, "ld%d" % (4 + i)) for i in range(NQ)]
            NP = 4
            Pt = [arena.alloc([512], BF16) for _ in range(NP)]
            PtB = [Buf("p2_p%d" % i) for i in range(NP)]
            RC = arena.alloc([512], F32)
            RCB = Buf("p2_rc")
            YS = [arena.alloc([512], BF16) for _ in range(2)]
            YSB = [Buf("p2_y%d" % i, "st%d" % i) for i in range(2)]
            SBK = [0, 1, 2, 3]
            OBK = [4, 5]
            for i in range(2):
                P.op("pool", lambda e, i=i: e.memset(Vh[i][:, :, 64:128], 1.0), writes=[VB[i]])
                P.op("pool", lambda e, i=i: e.memset(KTh[i][64:65, :], 1.0), writes=[KB[i]])

            def load_head(h):
                i = h % 2
                P.dma("sp", KTh[i][0:64, :], KT[h], writes=[KB[i]])
                P.dma("sp", Vh[i][:, :, 0:64], VS[h].rearrange("(b p) d -> p b d", p=128), writes=[VB[i]])

            items = []
            for h in range(H):
                for qt in range(NT):
                    for kb in range(4 * qt + 4):
                        items.append((h, qt, kb))
            qslot = {}
            qcount = [0]

            def load_q(h, qt):
                i = qcount[0] % NQ
                qcount[0] += 1
                qslot[(h, qt)] = i
                P.dma("sp", Qt[i][0:64, :], QT[h][:, qt * 512:(qt + 1) * 512], writes=[QB[i]])
                P.dma("sp", Qt[i][64:65, :], CUMB[h:h + 1, qt * 512:(qt + 1) * 512], writes=[QB[i]])

            def emit_qk(idx):
                h, qt, kb = items[idx]
                if idx == 0:
                    load_head(0)
                    load_head(1)
                if kb == 0:
                    if (h, qt) not in qslot:
                        load_q(h, qt)
                    nh, nqt = (h, qt + 1) if qt + 1 < NT else (h + 1, 0)
                    if nh < H and (nh, nqt) not in qslot:
                        load_q(nh, nqt)
                qi = qslot[(h, qt)]
                sb = SBK[idx % 4]
                j = kb - 4 * qt
                q0 = 128 * j if j > 0 else 0
                hi = h % 2
                diag = j >= 0
                P.op("pe", lambda e: e.matmul(psum[sb][:, q0:512], KTh[hi][0:65, kb * 128:(kb + 1) * 128],
                                              Qt[qi][0:65, q0:512], start=True, stop=not diag),
                     reads=[KB[hi], QB[qi]], writes=[PB[sb]], signal=not diag)
                if diag:
                    P.op("pe", lambda e: e.matmul(psum[sb][:, q0:512], ident_b[:, :], masks[:, j, q0:512],
                                                  start=False, stop=True),
                         reads=[B_const], writes=[PB[sb]], signal=True)

            def emit_rest(idx):
                h, qt, kb = items[idx]
                sb = SBK[idx % 4]
                pi = idx % NP
                j = kb - 4 * qt
                q0 = 128 * j if j > 0 else 0
                hi = h % 2
                oi = (h * NT + qt) % 2
                ob = OBK[oi]
                last = (kb == 4 * qt + 3)
                P.op("act", lambda e: e.activation(out=Pt[pi][:, q0:512], in_=psum[sb][:, q0:512], func=AF.Exp,
                                                   bias=nck[:, kb, h:h + 1], scale=1.0),
                     reads=[PB[sb], B_nck], writes=[PtB[pi]])
                P.op("pe", lambda e: e.matmul(psum[ob][:, q0:512], Vh[hi][:, kb, :], Pt[pi][:, q0:512],
                                              start=(kb == 0), stop=last),
                     reads=[VB[hi], PtB[pi]], writes=[PB[ob]], signal=True)
                if last:
                    P.op("dve", lambda e: e.reciprocal(out=RC[64:128, :], in_=psum[ob][64:128, :]),
                         reads=[PB[ob]], writes=[RCB])
                    P.op("dve", lambda e: e.tensor_tensor(out=YS[oi][0:64, :], in0=psum[ob][0:64, :], in1=RC[64:128, :],
                                                          op=ALU.mult), reads=[PB[ob], RCB], writes=[YSB[oi]])
                    P.dma("sp", YAs[s][h][:, qt * 512:(qt + 1) * 512], YS[oi][0:64, :], reads=[YSB[oi]], sem_buf=YSB[oi])
                    if qt == NT - 1 and h + 2 < H:
                        load_head(h + 2)

            LOOK = 2
            n = len(items)
            per = 0
            late_calls = []
            if side_calls:
                early = []
                late = False
                for c in side_calls:
                    if c[0] == "mark":
                        late = (c[1][0] == "late_begin")
                        continue
                    (late_calls if late else early).append(c)
                side_calls = early
                per = (len(side_calls) + int(0.7 * n) - 1) // max(1, int(0.7 * n))
            for idx in range(min(LOOK, n)):
                emit_qk(idx)
            for idx in range(n):
                if idx + LOOK < n:
                    emit_qk(idx + LOOK)
                emit_rest(idx)
                if side_calls:
                    replay(side_calls, per)
            if side_calls:
                replay(side_calls, len(side_calls))
            if late_calls:
                replay(late_calls, len(late_calls))
            P.barrier()

        def bc_last(ap2, n):
            return ap2.unsqueeze(2).to_broadcast([ap2.shape[0], ap2.shape[1], n])

        def ssm_setup(l, PP, arena, banks, overlapped):
            import math
            arena.reset()
            TB = Buf("ssm_tab")
            stage = arena.alloc([128], F32)
            B_stage = Buf("sstage", "sx0")

            def V(fn, eng="dve"):
                PP.op(eng, fn, reads=[TB], writes=[TB])

            def tt(out, a, b, op, eng="dve"):
                V(lambda e: e.tensor_tensor(out=out, in0=a, in1=b, op=op), eng)

            def load_cols(src2d, R, dst):
                PP.dma("sp", stage[0:R, :], src2d, writes=[B_stage])
                PP.op("pe", lambda e: e.matmul(psum[banks[0]][:, 0:R], stage[0:R, :], ident_f[0:R, 0:R], start=True, stop=True),
                     reads=[B_stage, B_const], writes=[PB[banks[0]]])
                PP.op("dve", lambda e: e.tensor_copy(out=dst, in_=psum[banks[0]][:, 0:R]), reads=[PB[banks[0]], TB], writes=[TB])

            def ctab(LRE, LIM, LDT, shp, npow):
                T = {}
                def new(nm):
                    T[nm] = arena.alloc(shp, F32)
                    return T[nm]
                dt = new("dt"); a = new("a"); th = new("th"); mag = new("mag"); t = new("t")
                sv = new("sv"); cv = new("cv"); lbr = new("lbr"); lbi = new("lbi"); nr = new("nr")
                den = new("den"); kr = new("kr"); ki = new("ki")
                V(lambda e: e.activation(out=dt, in_=LDT, func=AF.Exp), "act")
                tt(a, LRE, dt, ALU.mult)
                tt(th, LIM, dt, ALU.mult)
                V(lambda e: e.activation(out=mag, in_=a, func=AF.Exp), "act")
                for (xv, sh) in ((sv, math.pi), (cv, 1.5 * math.pi)):
                    V(lambda e, xv=xv, sh=sh: e.tensor_scalar(out=xv, in0=th, scalar1=sh, scalar2=None, op0=ALU.add))
                    for _ in range(5):
                        V(lambda e, xv=xv: e.tensor_scalar(out=t, in0=xv, scalar1=2 * math.pi, scalar2=2 * math.pi,
                                                           op0=ALU.is_ge, op1=ALU.mult))
                        tt(xv, xv, t, ALU.subtract)
                    V(lambda e, xv=xv: e.tensor_scalar(out=xv, in0=xv, scalar1=-math.pi, scalar2=None, op0=ALU.add))
                    V(lambda e, xv=xv: e.tensor_scalar(out=xv, in0=xv, scalar1=math.pi, scalar2=-math.pi, op0=ALU.min, op1=ALU.max))
                    V(lambda e, xv=xv: e.activation(out=xv, in_=xv, func=AF.Sin), "act")
                tt(lbr, mag, cv, ALU.mult)
                tt(lbi, mag, sv, ALU.mult)
                V(lambda e: e.tensor_scalar(out=nr, in0=lbr, scalar1=-1.0, scalar2=None, op0=ALU.add))
                tt(den, LRE, LRE, ALU.mult)
                tt(t, LIM, LIM, ALU.mult)
                tt(den, den, t, ALU.add)
                V(lambda e: e.reciprocal(out=den, in_=den))
                tt(kr, nr, LRE, ALU.mult)
                tt(t, lbi, LIM, ALU.mult)
                tt(kr, kr, t, ALU.add)
                tt(kr, kr, den, ALU.mult)
                tt(ki, lbi, LRE, ALU.mult)
                tt(t, nr, LIM, ALU.mult)
                tt(ki, ki, t, ALU.subtract)
                tt(ki, ki, den, ALU.mult)
                pwr = [new("pwr%d" % k) for k in range(npow + 1)]
                pwi = [new("pwi%d" % k) for k in range(npow + 1)]
                V(lambda e: e.memset(pwr[0], 1.0))
                V(lambda e: e.memset(pwi[0], 0.0))

                def cmul(outr, outi, ar, ai, br, bi):
                    tt(outr, ar, br, ALU.mult)
                    tt(t, ai, bi, ALU.mult)
                    tt(outr, outr, t, ALU.subtract)
                    tt(outi, ar, bi, ALU.mult)
                    tt(t, ai, br, ALU.mult)
                    tt(outi, outi, t, ALU.add)
                for k in range(1, npow + 1):
                    cmul(pwr[k], pwi[k], pwr[k - 1], pwi[k - 1], lbr, lbi)
                T["pwr"] = pwr; T["pwi"] = pwi; T["cmul"] = cmul
                return T

            LREp = arena.alloc([16], F32); LIMp = arena.alloc([16], F32); LDTp = arena.alloc([16], F32)
            load_cols(lam_re[l].rearrange("(a b) -> a b", b=128), 16, LREp)
            load_cols(lam_im[l].rearrange("(a b) -> a b", b=128), 16, LIMp)
            ld2 = arena.alloc([2], F32)
            B_ld2 = Buf("sld2", "sx1")
            PP.dma("sp", ld2[0:16, :], log_dt[l].rearrange("(a b) -> a b", b=2), writes=[B_ld2])
            PP.op("dve", lambda e: e.tensor_copy(out=stage[0:16, :].rearrange("p (a b) -> p a b", a=2),
                                                in_=bc_last(ld2[0:16, :], 64)), reads=[B_ld2, B_stage], writes=[B_stage])
            PP.op("pe", lambda e: e.matmul(psum[banks[0]][:, 0:16], stage[0:16, :], ident_f[0:16, 0:16], start=True, stop=True),
                 reads=[B_stage, B_const], writes=[PB[banks[0]]])
            PP.op("dve", lambda e: e.tensor_copy(out=LDTp, in_=psum[banks[0]][:, 0:16]), reads=[PB[banks[0]], TB], writes=[TB])
            Tp = ctab(LREp, LIMp, LDTp, [16], LCH)
            pwr, pwi, cmul = Tp["pwr"], Tp["pwi"], Tp["cmul"]
            Xre = arena.alloc([16, 32], F32); Xim = arena.alloc([16, 32], F32)
            V(lambda e: e.memset(Xre, 0.0), "pool")
            V(lambda e: e.memset(Xim, 0.0), "pool")
            for (X, src) in ((Xre, b_re), (Xim, b_im)):
                v = src[l].rearrange("(a m p c) -> m p a c", m=2, p=64, c=16)
                for m in range(2):
                    PP.dma("sp", X[m * 64:(m + 1) * 64, :, m * 16:(m + 1) * 16], v[m], reads=[TB], writes=[TB], sem_buf=B_stage)
            AX = arena.alloc([LCH, 2, 16, 32], BF16)
            Fr = arena.alloc([16], F32); Fi = arena.alloc([16], F32)
            t1 = arena.alloc([16, 32], F32); t2 = arena.alloc([16, 32], F32)
            for k in range(LCH):
                cmul(Fr, Fi, pwr[k], pwi[k], Tp["kr"], Tp["ki"])
                tt(t1, Xre, bc_last(Fr, 32), ALU.mult)
                tt(t2, Xim, bc_last(Fi, 32), ALU.mult, "pool")
                tt(AX[:, k, 0, :, :], t1, t2, ALU.subtract)
                tt(t1, Xim, bc_last(Fr, 32), ALU.mult)
                tt(t2, Xre, bc_last(Fi, 32), ALU.mult, "pool")
                tt(AX[:, k, 1, :, :], t1, t2, ALU.add)
            PP.mark("late_begin")
            SA = [arena.alloc([LCH * 128], BF16) for _ in range(2)]
            SAB = [Buf("ssm_sa%d" % i, "sx%d" % (2 + i)) for i in range(2)]
            AMv = AM[l].rearrange("r (k x) -> r k x", x=LCH * 128)
            cnt = 0
            for pi in range(16):
                kc, q = divmod(pi, 4)
                for part in range(2):
                    sa = SA[cnt % 2]; saB = SAB[cnt % 2]; cnt += 1
                    for jb in range(4):
                        pb = banks[(cnt * 4 + jb) % len(banks)]
                        for jj in range(4):
                            j = jb * 4 + jj
                            PP.op("pe", lambda e, pb=pb, jj=jj, j=j, pi=pi, part=part: e.matmul(
                                psum[pb][0:32, jj * 128:(jj + 1) * 128], AX[:, LCH - 1 - j, part, pi, :], ident_b[:, :],
                                start=True, stop=True), reads=[TB, B_const], writes=[PB[pb]], signal=(jj == 3))
                        eng = "act" if (jb % 2 == 0 and not overlapped) else "dve"
                        if eng == "act":
                            PP.op("act", lambda e, pb=pb, jb=jb, sa=sa: e.activation(out=sa[0:32, jb * 512:(jb + 1) * 512],
                                                                                    in_=psum[pb][0:32, :], func=AF.Copy),
                                 reads=[PB[pb]], writes=[saB])
                        else:
                            PP.op("dve", lambda e, pb=pb, jb=jb, sa=sa: e.tensor_copy(out=sa[0:32, jb * 512:(jb + 1) * 512],
                                                                                     in_=psum[pb][0:32, :]),
                                 reads=[PB[pb]], writes=[saB])
                    PP.dma("sp", AMv[32 * q:32 * q + 32, kc * 2 + part, :], sa[0:32, :], reads=[saB], sem_buf=saB)
            PP.mark("late_end")
            rho = arena.alloc([16], F32); irho = arena.alloc([16], F32)
            Mr = arena.alloc([16], F32); Mi = arena.alloc([16], F32); Mt = arena.alloc([16], F32)
            V(lambda e: e.activation(out=rho, in_=Tp["a"], func=AF.Exp, scale=float(LCH)), "act")
            V(lambda e: e.reciprocal(out=irho, in_=rho))
            tt(Mr, pwr[LCH], irho, ALU.mult)
            tt(Mi, pwi[LCH], irho, ALU.mult)
            mark_tab = arena.off
            EC = arena.alloc([16, NCH], F32); ES = arena.alloc([16, NCH], F32); RH = arena.alloc([16, NCH], F32)
            u1 = arena.alloc([16, NCH // 2], F32); u2 = arena.alloc([16, NCH // 2], F32)
            V(lambda e: e.memset(EC[:, :, 0:1], 1.0))
            V(lambda e: e.memset(ES[:, :, 0:1], 0.0))
            nst = NCH.bit_length() - 1
            for k in range(nst):
                w = 1 << k
                if k > 0:
                    tt(Mt, Mr, Mi, ALU.mult)
                    tt(Mr, Mr, Mr, ALU.mult)
                    tt(Mi, Mi, Mi, ALU.mult)
                    tt(Mr, Mr, Mi, ALU.subtract)
                    V(lambda e: e.tensor_scalar(out=Mi, in0=Mt, scalar1=2.0, scalar2=None, op0=ALU.mult))
                tt(u1[:, :, 0:w], EC[:, :, 0:w], bc_last(Mr, w), ALU.mult)
                tt(u2[:, :, 0:w], ES[:, :, 0:w], bc_last(Mi, w), ALU.mult, "pool")
                tt(EC[:, :, w:2 * w], u1[:, :, 0:w], u2[:, :, 0:w], ALU.subtract)
                tt(u1[:, :, 0:w], EC[:, :, 0:w], bc_last(Mi, w), ALU.mult)
                tt(u2[:, :, 0:w], ES[:, :, 0:w], bc_last(Mr, w), ALU.mult, "pool")
                tt(ES[:, :, w:2 * w], u1[:, :, 0:w], u2[:, :, 0:w], ALU.add)
            V(lambda e: e.tensor_copy(out=RH, in_=bc_last(rho, NCH)))
            V(lambda e: e.memset(RH[:, :, 0:1], 0.0))
            ETv = ET[l].rearrange("r (k a n) -> r k a n", k=3, a=16)
            for i, tl in enumerate((EC, ES, RH)):
                PP.dma("sp", ETv[:, i, :, :], tl, reads=[TB], writes=[TB], sem_buf=B_stage)
            arena.off = mark_tab

            LREf = arena.alloc([4, 64], F32); LIMf = arena.alloc([4, 64], F32); LDTf = arena.alloc([4, 64], F32)
            LDTs = arena.alloc([4], F32)
            for gl in range(8):
                for (dst, src, wid) in ((LREf, lam_re, 64), (LIMf, lam_im, 64)):
                    sap = bass.AP(src.tensor, src[l].offset + gl * 64, [[0, 16], [512, 4], [1, 64]])
                    PP.dma("sp", dst[gl * 16:(gl + 1) * 16, :, :], sap, reads=[TB], writes=[TB], sem_buf=B_stage)
                sap = bass.AP(log_dt.tensor, log_dt[l].offset + gl, [[0, 16], [8, 4], [1, 1]])
                PP.dma("sp", LDTs[gl * 16:(gl + 1) * 16, :].unsqueeze(2), sap, reads=[TB], writes=[TB], sem_buf=B_stage,
                      allow_slow_non_contiguous=True)
            Cre = arena.alloc([4, 64], F32); Cim = arena.alloc([4, 64], F32)
            PP.dma("sp", Cre, c_re[l].rearrange("(k r p) -> r k p", k=4, p=64), reads=[TB], writes=[TB], sem_buf=B_stage)
            PP.dma("sp", Cim, c_im[l].rearrange("(k r p) -> r k p", k=4, p=64), reads=[TB], writes=[TB], sem_buf=B_stage)
            V(lambda e: e.tensor_copy(out=LDTf, in_=bc_last(LDTs, 64)))
            Tf = ctab(LREf, LIMf, LDTf, [4, 64], LCH)
            fr, fi = Tf["pwr"], Tf["pwi"]
            CL = arena.alloc([LCH + 1, 2, 4, 64], BF16)
            v1 = arena.alloc([4, 64], F32); v2 = arena.alloc([4, 64], F32)
            for k in range(LCH + 1):
                tt(v1, Cre, fr[k], ALU.mult)
                tt(v2, Cim, fi[k], ALU.mult, "pool")
                tt(CL[:, k, 0, :, :], v1, v2, ALU.subtract)
                tt(v1, Cre, fi[k], ALU.mult)
                tt(v2, Cim, fr[k], ALU.mult, "pool")
                V(lambda e, k=k: e.scalar_tensor_tensor(out=CL[:, k, 1, :, :], in0=v1, scalar=-1.0, in1=v2,
                                                         op0=ALU.mult, op1=ALU.subtract))
            zt = arena.alloc([4 * LCH * 128], BF16)
            V(lambda e: e.memset(zt, 0.0), "pool")
            PP.dma("sp", KM[l], zt, reads=[TB], writes=[TB], sem_buf=B_stage)
            PP.mark("late_begin")
            SC = [[arena.alloc([LCH + 1, 32], BF16) for _ in range(2)] for _ in range(2)]
            SCB = [Buf("ssm_sc%d" % i, "sx%d" % (4 + i)) for i in range(2)]
            for i in range(2):
                for part in range(2):
                    PP.op("pool", lambda e, i=i, part=part: e.memset(SC[i][part], 0.0), writes=[SCB[i]])
            SK = [arena.alloc([LCH, 32], BF16) for _ in range(2)]
            SKB = [Buf("ssm_sk%d" % i, "sx%d" % (6 + i)) for i in range(2)]
            CMv = CM[l].rearrange("r (a b x) -> r a b x", a=16, b=2)
            KMv = KM[l].rearrange("r (k t c) -> r k t c", k=4, t=LCH)
            for pi in range(16):
                kc, q = divmod(pi, 4)
                sl = pi % 2
                for part in range(2):
                    pa = banks[(pi * 2 + part) * 2 % len(banks)]
                    pb2 = banks[((pi * 2 + part) * 2 + 1) % len(banks)]
                    for k in range(LCH + 1):
                        bank = pa if k < LCH else pb2
                        col = (k % LCH) * 32
                        for mh in range(2):
                            PP.op("pe", lambda e, bank=bank, col=col, mh=mh, k=k, part=part, kc=kc, q=q: e.matmul(
                                psum[bank][mh * 64:(mh + 1) * 64, col:col + 32],
                                CL[32 * q:32 * q + 32, k, part, kc, :],
                                ident_b[32 * q:32 * q + 32, 32 * q:32 * q + 32],
                                start=True, stop=True, tile_position=(32 * q, 64 * mh)),
                                reads=[TB, B_const], writes=[PB[bank]], signal=(mh == 1 and (k == LCH - 1 or k == LCH)))
                    sc = SC[sl][part]
                    for mh in range(2):
                        eng = "act" if (mh == 0 and not overlapped) else "dve"
                        src = psum[pa][mh * 64:(mh + 1) * 64, :].rearrange("p (k c) -> p k c", c=32)[:, :, mh * 16:(mh + 1) * 16]
                        dst = sc[mh * 64:(mh + 1) * 64, 0:LCH, mh * 16:(mh + 1) * 16]
                        src2 = psum[pb2][mh * 64:(mh + 1) * 64, mh * 16:(mh + 1) * 16]
                        dst2 = sc[mh * 64:(mh + 1) * 64, LCH, mh * 16:(mh + 1) * 16]
                        if eng == "act":
                            PP.op("act", lambda e, src=src, dst=dst: e.activation(out=dst, in_=src, func=AF.Copy),
                                 reads=[PB[pa]], writes=[SCB[sl]])
                            PP.op("act", lambda e, src2=src2, dst2=dst2: e.activation(out=dst2, in_=src2, func=AF.Copy),
                                 reads=[PB[pb2]], writes=[SCB[sl]])
                        else:
                            PP.op("dve", lambda e, src=src, dst=dst: e.tensor_copy(out=dst, in_=src),
                                 reads=[PB[pa]], writes=[SCB[sl]])
                            PP.op("dve", lambda e, src2=src2, dst2=dst2: e.tensor_copy(out=dst2, in_=src2),
                                 reads=[PB[pb2]], writes=[SCB[sl]])
                    PP.dma("sp", CMv[:, pi, part, :], sc.rearrange("p k c -> p (k c)"), reads=[SCB[sl]], sem_buf=SCB[sl])
                pk = banks[-1]
                for tau in range(LCH):
                    PP.op("pe", lambda e, tau=tau, pi=pi, sl=sl: e.matmul(psum[pk][0:32, tau * 32:(tau + 1) * 32],
                                                                         AX[:, 0, 0, pi, :], SC[sl][0][:, tau, :],
                                                                         start=True, stop=False),
                         reads=[TB, SCB[sl]], writes=[PB[pk]], signal=False)
                    PP.op("pe", lambda e, tau=tau, pi=pi, sl=sl: e.matmul(psum[pk][0:32, tau * 32:(tau + 1) * 32],
                                                                         AX[:, 0, 1, pi, :], SC[sl][1][:, tau, :],
                                                                         start=False, stop=True),
                         reads=[TB, SCB[sl]], writes=[PB[pk]], signal=(tau == LCH - 1))
                sk = SK[sl]
                PP.op("dve", lambda e, sk=sk: e.tensor_copy(out=sk[0:32, :, :],
                                                           in_=psum[pk][0:32, :].rearrange("p (t c) -> p t c", c=32)),
                     reads=[PB[pk]], writes=[SKB[sl]])
                PP.dma("sp", KMv[32 * q:32 * q + 32, kc, :, 32 * q:32 * q + 32], sk[0:32, :, :], reads=[SKB[sl], TB],
                      sem_buf=SKB[sl])
            PP.barrier()

        def phase3(l, s):
            arena.reset()
            U2 = arena.alloc([4, LCH, NCH], BF16)
            UB = Buf("p3_u2")
            SPr = [arena.alloc([16, NCH], BF16) for _ in range(2)]
            SPB = Buf("p3_sp")
            SL = [arena.alloc([16, NCH], F32) for _ in range(2)]
            SLB = [Buf("p3_sl%d" % i) for i in range(2)]
            mark = arena.off
            Uraw = arena.alloc([4, S], BF16)
            UrB = [Buf("p3_ur%d" % c, "ld%d" % (3 + c)) for c in range(4)]
            usrc = UT.rearrange("(c p) t -> p c t", p=128)
            for c in range(4):
                P.dma("sp", Uraw[:, c, :], usrc[:, c, :], writes=[UrB[c]])
            for c in range(4):
                src = Uraw[:, c, :].rearrange("p (n j) -> p j n", j=LCH)
                if c % 2 == 0:
                    P.op("dve", lambda e, c=c, src=src: e.tensor_copy(out=U2[:, c, :, :], in_=src), reads=[UrB[c]], writes=[UB])
                else:
                    P.op("pool", lambda e, c=c, src=src: e.tensor_copy(out=U2[:, c, :, :], in_=src), reads=[UrB[c]], writes=[UB])
            Amat = arena.alloc([4, 2, LCH, 128], BF16)
            AB = Buf("p3_am", "ld1")
            P.dma("sp", Amat.rearrange("p a b c d -> p (a b c d)"), AM[l], writes=[AB])
            cnt = 0
            for pi in range(16):
                kc, q = divmod(pi, 4)
                for part in range(2):
                    pb = cnt % 8
                    cnt += 1
                    for j in range(LCH):
                        P.op("pe", lambda e, pb=pb, j=j, kc=kc, q=q, part=part: e.matmul(
                            psum[pb][:, 0:NCH], Amat[32 * q:32 * q + 32, kc, part, j, :],
                            U2[32 * q:32 * q + 32, kc, j, :], start=(j == 0), stop=(j == LCH - 1),
                            tile_position=(32 * q, 0)),
                            reads=[AB, UB], writes=[PB[pb]], signal=(j == LCH - 1))
                    if part == 0:
                        P.op("act", lambda e, pb=pb, pi=pi: e.activation(out=SL[0][:, pi, :], in_=psum[pb][:, 0:NCH],
                                                                         func=AF.Copy), reads=[PB[pb]], writes=[SLB[0]])
                    else:
                        P.op("dve", lambda e, pb=pb, pi=pi: e.tensor_copy(out=SL[1][:, pi, :], in_=psum[pb][:, 0:NCH]),
                             reads=[PB[pb]], writes=[SLB[1]])
            P.barrier()
            arena.off = mark
            EC = arena.alloc([16, NCH], F32); ES = arena.alloc([16, NCH], F32); RH = arena.alloc([16, NCH], F32)
            EB = Buf("p3_et", "ld1")
            ETv = ET[l].rearrange("r (k a n) -> r k a n", k=3, a=16)
            for i, tl in enumerate((EC, ES, RH)):
                P.dma("sp", tl, ETv[:, i, :, :], writes=[EB])
            Wr = arena.alloc([16, NCH], F32); Wi = arena.alloc([16, NCH], F32)
            m1 = arena.alloc([16, NCH], F32); m2 = arena.alloc([16, NCH], F32)
            WB_ = Buf("p3_w")

            def tt(out, a, b, op, eng="dve", reads=(), writes=()):
                P.op(eng, lambda e: e.tensor_tensor(out=out, in0=a, in1=b, op=op), reads=list(reads), writes=list(writes))
            Bm1, Bm2, BWr, BWi = Buf("p3_m1"), Buf("p3_m2"), Buf("p3_wr"), Buf("p3_wi")
            tt(Wr, EC, SL[0], ALU.mult, "dve", [EB, SLB[0]], [BWr])
            tt(m1, ES, SL[1], ALU.mult, "pool", [EB, SLB[1]], [Bm1])
            tt(Wr, Wr, m1, ALU.add, "dve", [BWr, Bm1], [BWr])
            tt(Wi, EC, SL[1], ALU.mult, "dve", [EB, SLB[1]], [BWi])
            tt(m2, ES, SL[0], ALU.mult, "pool", [EB, SLB[0]], [Bm2])
            tt(Wi, Wi, m2, ALU.subtract, "dve", [BWi, Bm2], [BWi])
            fl = "p a n -> p (a n)"
            P.op("dve", lambda e: e.tensor_tensor_scan(out=SL[0].rearrange(fl), data0=RH.rearrange(fl), data1=Wr.rearrange(fl),
                                                       initial=0.0, op0=ALU.mult, op1=ALU.add),
                 reads=[EB, BWr, Bm2], writes=[SLB[0]])
            P.op("dve", lambda e: e.tensor_tensor_scan(out=SL[1].rearrange(fl), data0=RH.rearrange(fl), data1=Wi.rearrange(fl),
                                                       initial=0.0, op0=ALU.mult, op1=ALU.add),
                 reads=[EB, BWi, Bm1], writes=[SLB[1]])
            n1 = NCH - 1
            P.op("pool", lambda e: e.memset(SPr[0][:, :, 0:1], 0.0), writes=[SPB])
            P.op("pool", lambda e: e.memset(SPr[1][:, :, 0:1], 0.0), writes=[SPB])
            tt(Wr[:, :, 0:n1], EC[:, :, 0:n1], SL[0][:, :, 0:n1], ALU.mult, "dve", [EB, SLB[0]], [BWr])
            tt(m1[:, :, 0:n1], ES[:, :, 0:n1], SL[1][:, :, 0:n1], ALU.mult, "pool", [EB, SLB[1]], [Bm1])
            tt(SPr[0][:, :, 1:NCH], Wr[:, :, 0:n1], m1[:, :, 0:n1], ALU.subtract, "dve", [BWr, Bm1], [SPB])
            tt(Wi[:, :, 0:n1], EC[:, :, 0:n1], SL[1][:, :, 0:n1], ALU.mult, "dve", [EB, SLB[1]], [BWi])
            tt(m2[:, :, 0:n1], ES[:, :, 0:n1], SL[0][:, :, 0:n1], ALU.mult, "pool", [EB, SLB[0]], [Bm2])
            tt(SPr[1][:, :, 1:NCH], Wi[:, :, 0:n1], m2[:, :, 0:n1], ALU.add, "dve", [BWi, Bm2], [SPB])
            P.barrier()
            arena.off = mark
            Cmat = arena.alloc([16, 2, LCH + 1, 32], BF16)
            CB = Buf("p3_cm", "ld1")
            P.dma("sp", Cmat.rearrange("p a b c d -> p (a b c d)"), CM[l], writes=[CB])
            Kmat = arena.alloc([4, LCH, 128], BF16)
            KB_ = Buf("p3_km", "ld2")
            P.dma("sp", Kmat.rearrange("p a b c -> p (a b c)"), KM[l], writes=[KB_])
            for kc in range(4):
                P.op("dve", lambda e, kc=kc: e.scalar_tensor_tensor(out=Kmat[:, kc, 0, :], in0=ident_f[:, :],
                                                                    scalar=dsk[:, l * 4 + kc:l * 4 + kc + 1],
                                                                    in1=Kmat[:, kc, 0, :], op0=ALU.mult, op1=ALU.add),
                     reads=[KB_, B_const], writes=[KB_])
            YS = [arena.alloc([S], BF16) for _ in range(2)]
            YSB = [Buf("p3_ys%d" % i, "st%d" % i) for i in range(2)]
            g1 = [arena.alloc([NCH], F32) for _ in range(2)]
            g2 = [arena.alloc([NCH], F32) for _ in range(2)]
            g1B = [Buf("p3_g1%d" % i) for i in range(2)]
            g2B = [Buf("p3_g2%d" % i) for i in range(2)]
            cnt = 0
            for kc in range(4):
                ys = YS[kc % 2]; ysB = YSB[kc % 2]
                for j in range(LCH):
                    pb = cnt % 8
                    gi = cnt % 2
                    cnt += 1
                    for i in range(j + 1):
                        P.op("pe", lambda e, pb=pb, i=i, j=j, kc=kc: e.matmul(
                            psum[pb][:, 0:NCH], Kmat[:, kc, j - i, :], U2[:, kc, i, :], start=(i == 0), stop=False),
                            reads=[KB_, UB], writes=[PB[pb]], signal=False)
                    for q in range(4):
                        pi = kc * 4 + q
                        for part in range(2):
                            last = (q == 3 and part == 1)
                            P.op("pe", lambda e, pb=pb, q=q, pi=pi, part=part, j=j, last=last: e.matmul(
                                psum[pb][32 * q:32 * q + 32, 0:NCH], Cmat[:, pi, part, j + 1, :], SPr[part][:, pi, :],
                                start=False, stop=(part == 1), tile_position=(0, 32 * q)),
                                reads=[CB, SPB], writes=[PB[pb]], signal=last)
                    P.op("act", lambda e, pb=pb, gi=gi: e.activation(out=g1[gi], in_=psum[pb][:, 0:NCH], func=AF.Square),
                         reads=[PB[pb]], writes=[g1B[gi]])
                    P.op("dve", lambda e, gi=gi: e.tensor_scalar(out=g1[gi], in0=g1[gi], scalar1=0.044715, scalar2=1.0,
                                                                 op0=ALU.mult, op1=ALU.add), reads=[g1B[gi]], writes=[g1B[gi]])
                    P.op("dve", lambda e, pb=pb, gi=gi: e.tensor_tensor(out=g2[gi], in0=psum[pb][:, 0:NCH], in1=g1[gi],
                                                                        op=ALU.mult), reads=[PB[pb], g1B[gi]], writes=[g2B[gi]])
                    P.op("act", lambda e, gi=gi: e.activation(out=g2[gi], in_=g2[gi], func=AF.Sigmoid, scale=1.5957691216),
                         reads=[g2B[gi]], writes=[g2B[gi]])
                    P.op("dve", lambda e, pb=pb, gi=gi, j=j, ys=ys: e.tensor_tensor(out=ys[:, j:S:LCH], in0=psum[pb][:, 0:NCH],
                                                                                    in1=g2[gi], op=ALU.mult),
                         reads=[PB[pb], g2B[gi]], writes=[ysB])
                P.dma("sp", YBs[s][kc * 128:(kc + 1) * 128, :], ys, reads=[ysB], sem_buf=ysB)
            P.barrier()

        def phase4(l):
            arena.reset()
            WG = arena.alloc([4, 512], BF16)
            WA = arena.alloc([4, 1024], BF16)
            WB = arena.alloc([4, 1024], BF16)
            WO = arena.alloc([8, 1024], BF16)
            B_wg, B_wa, B_wb, B_wo = Buf("p4_wg", "w0"), Buf("p4_wa", "w1"), Buf("p4_wb", "w2"), Buf("p4_wo", "w3")
            P.dma("pool", WG, w_glu[l].rearrange("(c p) n -> p c n", p=128), writes=[B_wg])
            P.dma("pool", WA, w_ba[l].rearrange("(c p) n -> p c n", p=128), writes=[B_wa])
            P.dma("pool", WB, w_bb[l].rearrange("(c p) n -> p c n", p=128), writes=[B_wb])
            P.dma("pool", WO, w_out[l].rearrange("(c p) n -> p c n", p=128), writes=[B_wo])
            ya = [arena.alloc([4, 512], BF16) for _ in range(2)]
            yb = [arena.alloc([4, 512], BF16) for _ in range(2)]
            sg = [arena.alloc([16, 512], BF16) for _ in range(2)]
            xT = [arena.alloc([8, 512], F32) for _ in range(2)]
            yaB = [Buf("p4_ya%d" % i, "ld%d" % i) for i in range(2)]
            ybB = [Buf("p4_yb%d" % i, "ld%d" % (2 + i)) for i in range(2)]
            sgB = [Buf("p4_sg%d" % i, "ld%d" % (4 + i)) for i in range(2)]
            xB = [Buf("p4_x%d" % i, "ld%d" % (6 + i)) for i in range(2)]
            sgl = [arena.alloc([4, 512], BF16) for _ in range(2)]
            sglB = [Buf("p4_sgl%d" % i) for i in range(2)]
            yb2 = [arena.alloc([4, 512], BF16) for _ in range(2)]
            yb2B = [Buf("p4_yb2%d" % i) for i in range(2)]
            t1 = [arena.alloc([512], F32) for _ in range(2)]
            t1B = [Buf("p4_t1%d" % i) for i in range(2)]
            t2 = [arena.alloc([512], F32) for _ in range(2)]
            t2B = [Buf("p4_t2%d" % i) for i in range(2)]
            mixed = [arena.alloc([8, 512], BF16) for _ in range(2)]
            mixB = [Buf("p4_mix%d" % i) for i in range(2)]
            tiles = [(s, t) for s in range(SPC) for t in range(NT)]
            NTL = len(tiles)

            def load(i):
                s, t = tiles[i]
                k = i % 2
                ts = slice(t * 512, (t + 1) * 512)
                P.dma("sp", yb[k], YBs[s].rearrange("(c p) t -> p c t", p=128)[:, :, ts], writes=[ybB[k]])
                P.dma("sp", ya[k], YAs[s].rearrange("h d t -> (h d) t").rearrange("(c p) t -> p c t", p=128)[:, :, ts],
                      writes=[yaB[k]])
                P.dma("sp", sg[k], SGs[s].rearrange("(c p) t -> p c t", p=128)[:, :, ts], writes=[sgB[k]])
                P.dma("sp", xT[k], XT[s].rearrange("(c p) t -> p c t", p=128)[:, :, ts], writes=[xB[k]])
            pbi = [0]

            def nextbank():
                pbi[0] = (pbi[0] + 1) % 8
                return pbi[0]

            def glu(i):
                k = i % 2
                for n in range(4):
                    pb = nextbank()
                    for c in range(4):
                        P.op("pe", lambda e, pb=pb, c=c, n=n: e.matmul(psum[pb][:, :], WG[:, c, n * 128:(n + 1) * 128],
                                                                        yb[k][:, c, :], start=(c == 0), stop=(c == 3)),
                             reads=[B_wg, ybB[k]], writes=[PB[pb]], signal=(c == 3))
                    P.op("act", lambda e, pb=pb, n=n: e.activation(out=sgl[k][:, n, :], in_=psum[pb][:, :], func=AF.Sigmoid,
                                                                   bias=bglu[:, l * 4 + n:l * 4 + n + 1], scale=1.0),
                         reads=[PB[pb], B_const], writes=[sglB[k]])
                    P.op("pool", lambda e, n=n: e.tensor_tensor(out=yb2[k][:, n, :], in0=yb[k][:, n, :], in1=sgl[k][:, n, :],
                                                                op=ALU.mult), reads=[ybB[k], sglB[k]], writes=[yb2B[k]])

            def branches(i):
                k = i % 2
                for n in range(8):
                    pa = nextbank()
                    for c in range(4):
                        P.op("pe", lambda e, pa=pa, c=c, n=n: e.matmul(psum[pa][:, :], WA[:, c, n * 128:(n + 1) * 128],
                                                                        ya[k][:, c, :], start=(c == 0), stop=(c == 3)),
                             reads=[B_wa, yaB[k]], writes=[PB[pa]], signal=(c == 3))
                    pb = nextbank()
                    for c in range(4):
                        P.op("pe", lambda e, pb=pb, c=c, n=n: e.matmul(psum[pb][:, :], WB[:, c, n * 128:(n + 1) * 128],
                                                                        yb2[k][:, c, :], start=(c == 0), stop=(c == 3)),
                             reads=[B_wb, yb2B[k]], writes=[PB[pb]], signal=(c == 3))
                    j = n % 2
                    P.op("dve", lambda e, pa=pa, n=n, j=j: e.tensor_tensor(out=t1[j], in0=psum[pa][:, :], in1=sg[k][:, n, :],
                                                                          op=ALU.mult), reads=[PB[pa], sgB[k]], writes=[t1B[j]])
                    P.op("dve", lambda e, pb=pb, n=n, j=j: e.tensor_tensor(out=t2[j], in0=psum[pb][:, :], in1=sg[k][:, 8 + n, :],
                                                                          op=ALU.mult), reads=[PB[pb], sgB[k]], writes=[t2B[j]])
                    P.op("pool", lambda e, n=n, j=j: e.tensor_tensor(out=mixed[k][:, n, :], in0=t1[j], in1=t2[j], op=ALU.add),
                         reads=[t1B[j], t2B[j]], writes=[mixB[k]])

            def outproj(i):
                s, t = tiles[i]
                k = i % 2
                for n in range(8):
                    pb = nextbank()
                    for c in range(8):
                        P.op("pe", lambda e, pb=pb, c=c, n=n: e.matmul(psum[pb][:, :], WO[:, c, n * 128:(n + 1) * 128],
                                                                        mixed[k][:, c, :], start=(c == 0), stop=(c == 7)),
                             reads=[B_wo, mixB[k]], writes=[PB[pb]], signal=(c == 7))
                    P.op("dve", lambda e, pb=pb, n=n: e.tensor_tensor(out=xT[k][:, n, :], in0=psum[pb][:, :], in1=xT[k][:, n, :],
                                                                      op=ALU.add), reads=[PB[pb], xB[k]], writes=[xB[k]])
                P.dma("sp", XT[s].rearrange("(c p) t -> p c t", p=128)[:, :, t * 512:(t + 1) * 512], xT[k],
                      reads=[xB[k]], sem_buf=xB[k])

            load(0)
            glu(0)
            for i in range(NTL):
                if i + 1 < NTL:
                    load(i + 1)
                branches(i)
                if i + 1 < NTL:
                    glu(i + 1)
                outproj(i)
            P.barrier()

        def phase5(l):
            arena.reset()
            TM = 256
            WU = arena.alloc([8, DFF], BF16)
            WD = arena.alloc([32, D], BF16)
            B_wu = [Buf("p5_wu%d" % c, "w%d" % c) for c in range(8)]
            B_wd = [Buf("p5_wd%d" % c, "w%d" % (8 + c)) for c in range(4)]
            usrc = w_up[l].rearrange("(c p) n -> p c n", p=128)
            dsrc = w_down[l].rearrange("(c p) n -> p c n", p=128)
            for c in range(8):
                P.dma("pool", WU[:, c, :], usrc[:, c, :], writes=[B_wu[c]])
            for c in range(4):
                P.dma("pool", WD[:, c * 8:(c + 1) * 8, :], dsrc[:, c * 8:(c + 1) * 8, :], writes=[B_wd[c]])
            NX = 3
            xT = [arena.alloc([8, TM], F32) for _ in range(NX)]
            xB = [Buf("p5_x%d" % i, "ld%d" % i) for i in range(NX)]
            sq = arena.alloc([8, TM], BF16)
            sqB = Buf("p5_sq")
            rstd = [arena.alloc([TM], F32) for _ in range(2)]
            rstdB = [Buf("p5_rstd%d" % i) for i in range(2)]
            hT = [arena.alloc([8, TM], BF16) for _ in range(2)]
            hB = [Buf("p5_h%d" % i) for i in range(2)]
            rl = [arena.alloc([TM], BF16) for _ in range(2)]
            rlB = [Buf("p5_rl%d" % i) for i in range(2)]
            act = arena.alloc([32, TM], BF16)
            actB = [Buf("p5_act%d" % i) for i in range(32)]
            tiles = [(s, t) for s in range(SPC) for t in range(S // TM)]
            NTL = len(tiles)

            def load(i):
                s, t = tiles[i]
                P.dma("sp", xT[i % NX], XT[s].rearrange("(c p) t -> p c t", p=128)[:, :, t * TM:(t + 1) * TM],
                      writes=[xB[i % NX]])
            pbi = [0]

            def nextbank():
                pbi[0] = (pbi[0] + 1) % 8
                return pbi[0]

            def prologue(i):
                kx = i % NX
                k2 = i % 2
                pb = nextbank()
                P.op("act", lambda e: e.activation(out=sq, in_=xT[kx], func=AF.Square), reads=[xB[kx]], writes=[sqB])
                for c in range(8):
                    P.op("pe", lambda e, c=c, pb=pb: e.matmul(psum[pb][:, 0:TM], ones_b[:, :], sq[:, c, :],
                                                              start=(c == 0), stop=(c == 7)),
                         reads=[sqB, B_const], writes=[PB[pb]], signal=(c == 7))
                P.op("act", lambda e, pb=pb: e.activation(out=rstd[k2], in_=psum[pb][:, 0:TM], func=AF.Ln, bias=epsc[:, 0:1],
                                                          scale=1.0 / D), reads=[PB[pb], B_const], writes=[rstdB[k2]])
                P.op("act", lambda e: e.activation(out=rstd[k2], in_=rstd[k2], func=AF.Exp, scale=-0.5),
                     reads=[rstdB[k2]], writes=[rstdB[k2]])
                for c in range(8):
                    P.op("dve", lambda e, c=c: e.scalar_tensor_tensor(
                        out=hT[k2][:, c, :], in0=xT[kx][:, c, :], scalar=gmlp[:, l * 8 + c:l * 8 + c + 1], in1=rstd[k2],
                        op0=ALU.mult, op1=ALU.mult), reads=[xB[kx], rstdB[k2], B_const], writes=[hB[k2]])

            def body5(i):
                s, t = tiles[i]
                kx = i % NX
                k2 = i % 2
                for n in range(32):
                    pb = nextbank()
                    for c in range(8):
                        P.op("pe", lambda e, pb=pb, c=c, n=n: e.matmul(psum[pb][:, 0:TM], WU[:, c, n * 128:(n + 1) * 128],
                                                                        hT[k2][:, c, :], start=(c == 0), stop=(c == 7)),
                             reads=[B_wu[c], hB[k2]], writes=[PB[pb]], signal=(c == 7))
                    j = n % 2
                    P.op("act", lambda e, pb=pb, j=j: e.activation(out=rl[j], in_=psum[pb][:, 0:TM], func=AF.Relu),
                         reads=[PB[pb]], writes=[rlB[j]])
                    P.op("pool", lambda e, n=n, j=j: e.tensor_tensor(out=act[:, n, :], in0=rl[j], in1=rl[j], op=ALU.mult),
                         reads=[rlB[j]], writes=[actB[n]])
                for n in range(8):
                    pb = nextbank()
                    for c in range(32):
                        P.op("pe", lambda e, pb=pb, c=c, n=n: e.matmul(psum[pb][:, 0:TM], WD[:, c, n * 128:(n + 1) * 128],
                                                                        act[:, c, :], start=(c == 0), stop=(c == 31)),
                             reads=[B_wd[c // 8], actB[c]], writes=[PB[pb]], signal=(c == 31))
                    P.op("dve", lambda e, pb=pb, n=n: e.tensor_tensor(out=xT[kx][:, n, :], in0=psum[pb][:, 0:TM],
                                                                      in1=xT[kx][:, n, :], op=ALU.add),
                         reads=[PB[pb], xB[kx]], writes=[xB[kx]])
                P.dma("sp", XT[s].rearrange("(c p) t -> p c t", p=128)[:, :, t * TM:(t + 1) * TM], xT[kx],
                      reads=[xB[kx]], sem_buf=xB[kx])

            load(0)
            if NTL > 1:
                load(1)
            prologue(0)
            for i in range(NTL):
                if i + 2 < NTL:
                    load(i + 2)
                if i + 1 < NTL:
                    prologue(i + 1)
                body5(i)
            P.barrier()

        def phase_out():
            arena.reset()
            xT = [arena.alloc([8, 512], F32) for _ in range(2)]
            xB = [Buf("po_x%d" % i, "ld%d" % i) for i in range(2)]
            sq = arena.alloc([8, 512], BF16)
            sqB = Buf("po_sq")
            rstd = arena.alloc([512], F32)
            rstdB = Buf("po_rstd")
            yT = arena.alloc([8, 512], F32)
            yB = Buf("po_y")
            ot = [arena.alloc([D], F32) for _ in range(2)]
            otB = [Buf("po_o%d" % i, "st%d" % i) for i in range(2)]
            tiles = [(s, t) for s in range(SPC) for t in range(NT)]

            def load(i):
                s, t = tiles[i]
                P.dma("sp", xT[i % 2], XT[s].rearrange("(c p) t -> p c t", p=128)[:, :, t * 512:(t + 1) * 512],
                      writes=[xB[i % 2]])
            load(0)
            cntb = [0]

            def bodyo(i):
                s, t = tiles[i]
                k = i % 2
                if i + 1 < len(tiles):
                    load(i + 1)
                rms_stats(xT[k], xB[k], sq, sqB, rstd, rstdB, 0)
                for c in range(8):
                    P.op("dve", lambda e, c=c: e.scalar_tensor_tensor(
                        out=yT[:, c, :], in0=xT[k][:, c, :], scalar=gfin[:, c:c + 1], in1=rstd,
                        op0=ALU.mult, op1=ALU.mult), reads=[xB[k], rstdB, B_const], writes=[yB])
                for tb in range(4):
                    o = ot[cntb[0] % 2]
                    oB = otB[cntb[0] % 2]
                    cntb[0] += 1
                    for half in range(2):
                        pb = 1 + (2 * cntb[0] + half) % 4
                        for c4 in range(4):
                            c = half * 4 + c4
                            P.op("pe", lambda e, pb=pb, c=c, c4=c4, tb=tb: e.matmul(
                                psum[pb][:, c4 * 128:(c4 + 1) * 128], yT[:, c, tb * 128:(tb + 1) * 128], ident_f[:, :],
                                start=True, stop=True), reads=[yB, B_const], writes=[PB[pb]], signal=(c4 == 3))
                        if half == 0:
                            P.op("act", lambda e, pb=pb, o=o: e.activation(out=o[:, 0:512], in_=psum[pb][:, :], func=AF.Copy),
                                 reads=[PB[pb]], writes=[oB])
                        else:
                            P.op("dve", lambda e, pb=pb, o=o: e.tensor_copy(out=o[:, 512:1024], in_=psum[pb][:, :]),
                                 reads=[PB[pb]], writes=[oB])
                    r0 = s * S + t * 512 + tb * 128
                    P.dma("sp", out_ap[r0:r0 + 128, :], o, reads=[oB], sem_buf=oB)
            for i in range(len(tiles)):
                bodyo(i)
            P.barrier()

        setup_consts()
        phase_in()
        P2_BYTES = 48 * 1024
        arena_side = Arena(arena_t[:, P2_BYTES // 2:], ARENA_BYTES - P2_BYTES)
        ssm_setup(0, P, arena, [0, 1, 2, 3, 4, 5, 6, 7], False)
        for l in range(L):
            for s in range(SPC):
                phase1(l, s)
                side = None
                if s == SPC - 1 and l + 1 < L:
                    rec = Recorder()
                    arena_side.reset()
                    ssm_setup(l + 1, rec, arena_side, [6, 7], True)
                    side = rec.calls
                phase2(l, s, side)
                phase3(l, s)
            phase4(l)
            if "stop4" not in dbg:
                phase5(l)
        if "stop4" not in dbg:
            phase_out()
        P.emit(E)
    return nc


_FLAT = {"ssm_lambda_re": (G * PS,), "ssm_lambda_im": (G * PS,), "ssm_b_re": (G * PS * CG,), "ssm_b_im": (G * PS * CG,),
         "ssm_c_re": (G * CG * PS,), "ssm_c_im": (G * CG * PS,)}


def kernel(**inputs):
    x = np.asarray(inputs["x"])
    B, S, _ = x.shape
    L = np.asarray(inputs["w_in"]).shape[0]
    assert B == NCORES * SPC
    nc = build_program(S, L)
    shared = {}
    for k, v in inputs.items():
        if k == "x":
            continue
        v = np.ascontiguousarray(np.asarray(v, dtype=np.float32))
        if k == "norm_final":
            v = v.reshape(1, D)
        elif k in _FLAT:
            v = v.reshape((v.shape[0],) + _FLAT[k])
        shared[k] = v
    in_maps = []
    for c in range(NCORES):
        m = dict(shared)
        m["x"] = np.ascontiguousarray(x[SPC * c:SPC * (c + 1)].reshape(SPC * S, D).astype(np.float32))
        in_maps.append(m)
    res = run_bass_kernel_spmd(nc, in_maps, core_ids=list(range(NCORES)))
    outs = [np.asarray(r["out"]).reshape(SPC, S, D) for r in res.results]
    return np.concatenate(outs, axis=0).astype(np.float32)
```
